# Optimizing a Trainium2 kernel written in Bass

```python
import jax, jax.numpy as jnp
from jax import lax
import numpy as np

D_MODEL = 2048
BATCH = 8
SEQ = 2048
DEPTH = 1

N_MEM = 256
EPS = 1e-6

GLA_HEADS = 4
GLA_DV = D_MODEL // 2
GLA_DK = GLA_DV // 2
GLA_HK = GLA_DK // GLA_HEADS
GLA_HV = GLA_DV // GLA_HEADS
GLA_GATE_RANK = 16
GLA_GATE_NORM = 16.0
GLA_CHUNK = 64

POOL_WIDTH = D_MODEL // 2
POOL_WINDOWS = (2, 4, 8, 16)
POOL_GROUPS = len(POOL_WINDOWS)
POOL_GW = POOL_WIDTH // POOL_GROUPS

N_BRANCH = 2

CROSS_HEADS = 4
CROSS_HD = D_MODEL // CROSS_HEADS

D_FF = 256 * ((8 * D_MODEL // 3 + 255) // 256)
CONV_W = 3

OFF_K = GLA_DK
OFF_V = 2 * GLA_DK
OFF_R = OFF_V + GLA_DV
OFF_A = OFF_R + GLA_DV
OFF_P = OFF_A + GLA_GATE_RANK
OFF_G = OFF_P + POOL_WIDTH
D_IN = OFF_G + N_BRANCH * D_MODEL

kernel_name = "gla_pool_gated_hybrid_block"


def rms_norm(x, g):
    xf = x.astype(jnp.float32)
    y = xf * lax.rsqrt(jnp.mean(xf * xf, axis=-1, keepdims=True) + EPS)
    return (y * g.astype(jnp.float32)).astype(x.dtype)


def gla_chunked(q, k, v, log_a):
    B, H, T, dk = q.shape
    dv = v.shape[-1]
    C = GLA_CHUNK
    n = T // C

    def to_chunks(t):
        return jnp.moveaxis(t.reshape(B, H, n, C, t.shape[-1]), 2, 0)

    qc, kc, vc, gc = to_chunks(q), to_chunks(k), to_chunks(v), to_chunks(log_a)
    causal = jnp.tril(jnp.ones((C, C), dtype=bool))[:, :, None]

    def step(S, inp):
        qi, ki, vi, gi = inp
        b = jnp.cumsum(gi, axis=2)
        diff = b[:, :, :, None, :] - b[:, :, None, :, :]
        decay = jnp.exp(jnp.where(causal, diff, -jnp.inf))
        A = jnp.einsum('bhid,bhjd,bhijd->bhij', qi, ki, decay)
        o = (jnp.einsum('bhij,bhjv->bhiv', A, vi)
             + jnp.einsum('bhid,bhdv->bhiv', qi * jnp.exp(b), S))
        b_last = b[:, :, -1:, :]
        S = (jnp.exp(b_last[:, :, 0, :])[..., None] * S
             + jnp.einsum('bhjd,bhjv->bhdv', ki * jnp.exp(b_last - b), vi))
        return S, o

    S0 = jnp.zeros((B, H, dk, dv), jnp.float32)
    _, o = lax.scan(step, S0, (qc, kc, vc, gc))
    return jnp.moveaxis(o, 0, 2).reshape(B, H, T, dv)


def multiscale_pool(p, w_pool, pool_scale):
    B, T, _ = p.shape
    pf = p.astype(jnp.float32)
    cs = jnp.concatenate([jnp.zeros((B, 1, POOL_WIDTH), jnp.float32),
                          jnp.cumsum(pf, axis=1)], axis=1)
    pos = jnp.arange(T)
    outs = []
    for gi, w in enumerate(POOL_WINDOWS):
        sl = slice(gi * POOL_GW, (gi + 1) * POOL_GW)
        start = jnp.maximum(pos + 1 - w, 0)
        cnt = (pos + 1 - start).astype(jnp.float32)
        window_sum = cs[:, 1:, sl] - cs[:, start, sl]
        outs.append(window_sum / cnt[None, :, None] - pf[:, :, sl])
    pooled = jnp.stack(outs, axis=2)
    mixed = jnp.einsum('btgc,gcd->btgd', pooled, w_pool.astype(jnp.float32))
    mixed = mixed.reshape(B, T, POOL_WIDTH) * pool_scale.astype(jnp.float32)
    return mixed.astype(p.dtype)


def hybrid_mixer(h, w_in, w_a2, b_a, g_gla, w_pool, pool_scale, w_branch, w_out):
    B, T, _ = h.shape
    f32 = jnp.float32
    proj = h @ w_in

    def heads(t, d):
        return t.reshape(B, T, GLA_HEADS, d).transpose(0, 2, 1, 3).astype(f32)

    q = heads(proj[..., :OFF_K], GLA_HK) * (GLA_HK ** -0.5)
    k = heads(proj[..., OFF_K:OFF_V], GLA_HK)
    v = heads(proj[..., OFF_V:OFF_R], GLA_HV)
    r = proj[..., OFF_R:OFF_A]
    gate_pre = (proj[..., OFF_A:OFF_P] @ w_a2 + b_a).astype(f32)
    log_a = heads(jax.nn.log_sigmoid(gate_pre) / GLA_GATE_NORM, GLA_HK)
    o = gla_chunked(q, k, v, log_a)
    o = o * lax.rsqrt(jnp.mean(o * o, axis=-1, keepdims=True) + EPS)
    o = o.transpose(0, 2, 1, 3).reshape(B, T, GLA_DV) * g_gla.astype(f32)
    o_gla = o.astype(h.dtype) * jax.nn.silu(r)

    o_pool = multiscale_pool(proj[..., OFF_P:OFF_G], w_pool, pool_scale)

    y_gla = o_gla @ w_branch[:GLA_DV]
    y_pool = o_pool @ w_branch[GLA_DV:]
    gates = jax.nn.sigmoid(proj[..., OFF_G:].astype(f32)).astype(h.dtype)
    merged = gates[..., :D_MODEL] * y_gla + gates[..., D_MODEL:] * y_pool
    return merged @ w_out


def memory_cross_attention(h, mem_n, w_cq, w_ckv, w_co):
    B, T, _ = h.shape
    M = mem_n.shape[1]
    q = (h @ w_cq).reshape(B, T, CROSS_HEADS, CROSS_HD)
    kv = (mem_n @ w_ckv).reshape(B, M, 2, CROSS_HEADS, CROSS_HD)
    k, v = kv[:, :, 0], kv[:, :, 1]
    s = jnp.einsum('bthd,bmhd->bhtm', q, k).astype(jnp.float32) * (CROSS_HD ** -0.5)
    pr = jax.nn.softmax(s, axis=-1).astype(v.dtype)
    o = jnp.einsum('bhtm,bmhd->bthd', pr, v).reshape(B, T, D_MODEL)
    return o @ w_co


def conv_glu_ffn(h, w_up, conv_w, conv_b, w_down):
    u = h @ w_up
    u = lax.conv_general_dilated(
        u, conv_w[:, None, :], window_strides=(1,), padding=[(CONV_W - 1, 0)],
        dimension_numbers=('NWC', 'WIO', 'NWC'), feature_group_count=2 * D_FF) + conv_b
    gate, val = u[..., :D_FF], u[..., D_FF:]
    return (jax.nn.silu(gate) * val) @ w_down


def setup_inputs(seed: int = 0) -> dict:
    key = jax.random.key(seed)
    ks = jax.random.split(key, 24)
    L, D = DEPTH, D_MODEL
    nrm = lambda k, shape, fan_in: jax.random.normal(k, shape, jnp.float32) * (fan_in ** -0.5)
    gain = lambda k, shape: 1.0 + 0.02 * jax.random.normal(k, shape, jnp.float32)
    return {
        "x": jax.random.normal(ks[0], (BATCH, SEQ, D), jnp.float32),
        "mem": jax.random.normal(ks[1], (BATCH, N_MEM, D), jnp.float32),
        "g_mix": gain(ks[2], (L, D)),
        "w_in": nrm(ks[3], (L, D, D_IN), D),
        "w_a2": nrm(ks[4], (L, GLA_GATE_RANK, GLA_DK), GLA_GATE_RANK),
        "b_a": 0.1 * jax.random.normal(ks[5], (L, GLA_DK), jnp.float32),
        "g_gla": gain(ks[6], (L, GLA_DV)),
        "w_pool": nrm(ks[7], (L, POOL_GROUPS, POOL_GW, POOL_GW), POOL_GW),
        "pool_scale": gain(ks[8], (L, POOL_WIDTH)),
        "w_branch": nrm(ks[9], (L, GLA_DV + POOL_WIDTH, D), GLA_DV),
        "w_out": nrm(ks[10], (L, D, D), D),
        "g_cross": gain(ks[11], (L, D)),
        "g_mem": gain(ks[12], (L, D)),
        "w_cq": nrm(ks[13], (L, D, D), D),
        "w_ckv": nrm(ks[14], (L, D, 2 * D), D),
        "w_co": nrm(ks[15], (L, D, D), D),
        "g_ffn": gain(ks[16], (L, D)),
        "w_up": nrm(ks[17], (L, D, 2 * D_FF), D),
        "conv_w": nrm(ks[18], (L, CONV_W, 2 * D_FF), CONV_W),
        "conv_b": 0.02 * jax.random.normal(ks[19], (L, 2 * D_FF), jnp.float32),
        "w_down": nrm(ks[20], (L, D_FF, D), D_FF),
        "g_final": gain(ks[21], (D,)),
    }


def reference(x, mem, g_mix, w_in, w_a2, b_a, g_gla, w_pool, pool_scale, w_branch, w_out,
              g_cross, g_mem, w_cq, w_ckv, w_co, g_ffn, w_up, conv_w, conv_b, w_down, g_final):
    for l in range(DEPTH):
        x = x + hybrid_mixer(rms_norm(x, g_mix[l]), w_in[l], w_a2[l], b_a[l], g_gla[l],
                             w_pool[l], pool_scale[l], w_branch[l], w_out[l])
        x = x + memory_cross_attention(rms_norm(x, g_cross[l]), rms_norm(mem, g_mem[l]),
                                       w_cq[l], w_ckv[l], w_co[l])
        x = x + conv_glu_ffn(rms_norm(x, g_ffn[l]), w_up[l], conv_w[l], conv_b[l], w_down[l])
    return rms_norm(x, g_final)
```

```python
import numpy as np
from contextlib import ExitStack
import concourse.bass as bass
import concourse.mybir as mybir
from concourse.bass_utils import run_bass_kernel_spmd

F32 = mybir.dt.float32
BF16 = mybir.dt.bfloat16
AF = mybir.ActivationFunctionType
ALU = mybir.AluOpType

D = 2048
T = 2048
TT = 512
NSUB = 4
KC = 16
NH = 4
HK = 128
HV = 256
DK = 512
DV = 1024
RANK = 16
PW = 1024
DFF = 5632
NMEM = 256
EPS = 1e-6
OFF_K = 512
OFF_V = 1024
OFF_R = 2048
OFF_A = 3072
OFF_P = 3088
OFF_G = 4112
SOFF_P = 3072
SOFF_G = 4096
D_IN = 8208
NSLOT = 5
NFC = DFF // 128

PC_GMIX, PC_GCROSS, PC_GMEM, PC_GFFN = 0, 16, 32, 48
PC_GGLA = 64
PC_PSCALE = 72
PC_CW0 = 80
PC_CW1 = PC_CW0 + 88
PC_CW2 = PC_CW1 + 88
PC_CB = PC_CW2 + 88
PC_N = PC_CB + 88
CC_IDENT = 0
CC_MASK4 = 128
CC_TRIU = 128 + 512
CC_ONES = CC_TRIU + 128
CC_INVC = CC_ONES + 128
CC_N = CC_INVC + 64

A_R1 = 0
A_R2 = 16896
A_R3 = A_R2 + 16384
A_R4 = A_R3 + 16384
A_R5 = A_R4 + 8192
A_R6 = A_R5 + 8192
A_R8 = A_R6 + 8192
A_END = A_R8 + 2048


class Tok:
    __slots__ = ("w", "r", "name")

    def __init__(self, name=""):
        self.w = {}
        self.r = {}
        self.name = name


def tok_after(name, olds):
    t = Tok(name)
    for o in olds:
        evs = list(o.r.values()) + list(o.w.values())
        for ev in evs:
            k = id(ev[0])
            if k not in t.r or t.r[k][1] < ev[1]:
                t.r[k] = ev
    return t


class Prog:
    NOSELF = ("pe",)

    def __init__(self, nc, es):
        self.nc = nc
        self.es = es
        self.eng = {}
        for name in ("pe", "act", "dve", "pool", "sp"):
            sem = es.enter_context(nc.semaphore("c_" + name))
            self.eng[name] = dict(ops=[], n=0, known={}, sem=sem)
        self.dma_cnt = {}

    def dma_sem(self, name):
        s = self.es.enter_context(self.nc.semaphore(name))
        self.dma_cnt[id(s)] = [s, 0]
        return s

    def op(self, engine, fn, reads=(), writes=(), dma=None, nowaw=False):
        E = self.eng[engine]
        need = {}

        def add(ev):
            k = id(ev[0])
            if k not in need or need[k][1] < ev[1]:
                need[k] = ev

        for t in reads:
            for ev in t.w.values():
                add(ev)
        for t in writes:
            if not nowaw:
                for ev in t.w.values():
                    add(ev)
            for ev in t.r.values():
                add(ev)
        waits = []
        for k, (sem, val) in need.items():
            if sem is E["sem"] and engine in self.NOSELF:
                continue
            if E["known"].get(k, 0) >= val:
                continue
            E["known"][k] = val
            waits.append((sem, val))
        if dma is not None:
            c = self.dma_cnt[id(dma)]
            c[1] += 16
            ev = (dma, c[1])
            inc = (dma, 16)
        else:
            E["n"] += 1
            ev = (E["sem"], E["n"])
            inc = (E["sem"], 1)
        E["ops"].append((waits, fn, inc))
        k = id(ev[0])
        for t in reads:
            t.r[k] = ev
        for t in writes:
            if nowaw:
                t.w[k] = ev
            else:
                t.w = {k: ev}
                t.r = {}
        return ev

    def wait_events(self, engine, evs):
        E = self.eng[engine]
        waits = []
        for sem, val in evs:
            if E["known"].get(id(sem), 0) >= val:
                continue
            E["known"][id(sem)] = val
            waits.append((sem, val))
        E["ops"].append((waits, None, None))

    def emit(self):
        nc = self.nc
        with nc.Block() as block:
            def mk(name):
                def body(e):
                    for waits, fn, inc in self.eng[name]["ops"]:
                        for sem, val in waits:
                            e.wait_ge(sem, val)
                        if fn is not None:
                            ins = fn(e)
                            if inc is not None:
                                ins.then_inc(inc[0], inc[1])
                return body
            block.tensor(mk("pe"))
            block.scalar(mk("act"))
            block.vector(mk("dve"))
            block.gpsimd(mk("pool"))
            block.sync(mk("sp"))


def build(ntiles=4, stages=3, dbg=()):
    nc = bass.Bass("TRN2", target_bir_lowering=False)
    dram_in = lambda name, shape: nc.dram_tensor(name, list(shape), F32, kind="ExternalInput").ap()
    x_d = dram_in("x", [T, D])
    mem_d = dram_in("mem", [NMEM, D])
    NKH = {}

    def dram_w(name, K, N):
        nkh = (K + 1023) // 1024
        ap = dram_in(name, [(N // 512) * nkh, 128, 8 * 512])
        NKH[id(ap)] = nkh
        return ap
    w_in = dram_w("w_in", D, 8192)
    wa_d = dram_in("wa_in", [D, RANK])
    wa2_d = dram_in("wa2", [RANK + 1, DK])
    w_pool_d = dram_in("w_pool", [4, 256, 256])
    w_branch = dram_w("w_branch", D, D)
    w_out = dram_w("w_out", D, D)
    w_cq = dram_w("w_cq", D, D)
    w_ckv = dram_w("w_ckv", D, 2 * D)
    w_co = dram_w("w_co", D, D)
    w_up = dram_w("w_up", D, 2 * DFF)
    w_down = dram_w("w_down", DFF, D)
    gvec_d = dram_in("gvec", [5, D])
    params_d = dram_in("params", [128, PC_N])
    consts_d = dram_in("consts", [128, CC_N])
    y_d = nc.dram_tensor("y", [T, D], F32, kind="ExternalOutput").ap()
    dbg_out = {}

    with ExitStack() as es:
        P = Prog(nc, es)
        sb = lambda name, shape, dt: es.enter_context(nc.sbuf_tensor(name, list(shape), dt))

        xt = sb("xt", [128, NSUB, D], F32)
        hT = sb("hT", [128, KC, TT], BF16)
        ring = [sb(f"ring{i}", [128, 8, 512], BF16) for i in range(NSLOT)]
        xn = sb("xn", [128, D], BF16)
        KmT = sb("KmT", [128, KC, NMEM], BF16)
        Vm = sb("Vm", [128, 2, D], BF16)
        wpool = sb("wpool", [128, 4, 2, 256], BF16)
        S = sb("S", [128, NH, HV], F32)
        S_bf = sb("S_bf", [128, NH, HV], BF16)
        aT = sb("aT", [32, TT], F32)
        wa2 = sb("wa2s", [RANK + 1, DK], F32)
        wa = sb("wa", [128, KC, RANK], BF16)
        params = sb("params_s", [128, PC_N], F32)
        consts = sb("consts_s", [128, CC_N], F32)
        ident = sb("ident", [128, 128], BF16)
        ones_bf = sb("ones_bf", [128, 128], BF16)
        lutj = sb("lutj", [128, 2], F32)
        halo = sb("halo", [128, 8, 16], F32)
        chalo = sb("chalo", [128, 2 * NFC, 2], F32)
        dec = sb("dec", [128, NH, NSUB], F32)
        cb0 = sb("cb0", [128, 2 * NFC], F32)
        cb1 = sb("cb1", [128, 2 * NFC], F32)
        ssq = sb("ssq", [128, 8], F32)
        rstd = sb("rstd", [128, 8], F32)
        arena = sb("arena", [128, A_END // 2], BF16)
        arena32 = arena.bitcast(F32)

        banks = [es.enter_context(nc.psum_tensor(f"bank{i}", [128, 512], F32)) for i in range(8)]
        banks_bf = [b.bitcast(BF16) for b in banks]
        bank_tok = [Tok(f"bank{i}") for i in range(8)]
        bank_i = [0]

        def next_bank():
            i = bank_i[0]
            bank_i[0] = (i + 1) % 8
            return i

        def av(off, shape, dt):
            n = 1
            for s_ in shape:
                n *= s_
            if dt == BF16:
                assert off % 2 == 0
                ap = arena[:, off // 2: off // 2 + n]
            else:
                assert off % 4 == 0
                ap = arena32[:, off // 4: off // 4 + n]
            if len(shape) == 2:
                ap = ap.rearrange("p (a b) -> p a b", a=shape[0])
            elif len(shape) == 3:
                ap = ap.rearrange("p (a b c) -> p a b c", a=shape[0], b=shape[1])
            return ap

        region_toks = {}

        def retok(region, names):
            olds = region_toks.get(region, [])
            news = [tok_after(n, olds) for n in names]
            region_toks[region] = news
            return news if len(news) > 1 else news[0]

        tk = {n: Tok(n) for n in "hT xn KmT Vm wpool S S_bf aT wa2 wa params consts ident ones_bf halo chalo dec ssq rstd cb0 cb1".split()}
        x_tok = [Tok(f"x{s}") for s in range(NSUB)]
        tk_ssq = [Tok(f"ssq{s}") for s in range(8)]
        t_lutj = Tok("lutj")
        tk_rstd = [Tok(f"rstd{s}") for s in range(8)]
        ring_tok = [Tok(f"ring{i}") for i in range(NSLOT)]
        ring_sem = [P.dma_sem(f"s_ring{i}") for i in range(NSLOT)]
        ring_i = [0]
        st_sem = [P.dma_sem(f"s_st{i}") for i in range(NSLOT)]
        NSCR = 132
        wscr = nc.dram_tensor("wscr", [NSCR, 128, 8 * 512], BF16).ap()
        scr_tok = {}
        slab_seq = [0]
        cur_tile = [-1]
        cur_stage = [0]
        s_x = [P.dma_sem(f"s_x{s}") for s in range(NSUB)]
        s_m = {n: P.dma_sem("s_" + n) for n in ("params", "consts", "wa2", "wa", "wpool", "memt")}
        s_gbr = {r: P.dma_sem("s_gb" + r) for r in ("R1", "R4", "R6")}
        s_out = [P.dma_sem(f"s_out{s}") for s in range(NSUB)]
        s_dbg = P.dma_sem("s_dbg")
        out_events = []
        alt = [0]

        def act(out, in_, func, reads, writes, **kw):
            P.op("act", lambda e: e.activation(out=out, in_=in_, func=func, **kw), reads, writes)

        def tt(eng, out, in0, in1, op, reads, writes):
            P.op(eng, lambda e: e.tensor_tensor(out=out, in0=in0, in1=in1, op=op), reads, writes)

        def stt(eng, out, in0, scalar, in1, op0, op1, reads, writes):
            P.op(eng, lambda e: e.scalar_tensor_tensor(out=out, in0=in0, scalar=scalar, in1=in1, op0=op0, op1=op1), reads, writes)

        def ts(eng, out, in0, s1, s2, op0, op1, reads, writes):
            if s2 is None:
                P.op(eng, lambda e: e.tensor_scalar(out=out, in0=in0, scalar1=s1, scalar2=None, op0=op0), reads, writes)
            else:
                P.op(eng, lambda e: e.tensor_scalar(out=out, in0=in0, scalar1=s1, scalar2=s2, op0=op0, op1=op1), reads, writes)

        def cp(eng, out, in_, reads, writes, nowaw=False):
            if eng == "act":
                P.op("act", lambda e: e.activation(out=out, in_=in_, func=AF.Copy), reads, writes, nowaw=nowaw)
            else:
                P.op(eng, lambda e: e.tensor_copy(out=out, in_=in_), reads, writes, nowaw=nowaw)

        def cp_any(out, in_, reads, writes):
            alt[0] ^= 1
            cp("act" if alt[0] else "dve", out, in_, reads, writes, nowaw=True)

        def mm(out, lhsT, rhs, start, stop, reads, writes):
            P.op("pe", lambda e: e.matmul(out, lhsT=lhsT, rhs=rhs, start=start, stop=stop), reads, writes)

        def tr(out, in_, reads, writes):
            P.op("pe", lambda e: e.transpose(out=out, in_=in_, identity=ident[:]), list(reads) + [tk["ident"]], writes)

        def dma(eng, out, in_, reads, writes, sem, **kw):
            return P.op(eng, lambda e: e.dma_start(out=out, in_=in_, **kw), reads, writes, dma=sem)

        def dump(name, ap_s, toks, shape):
            d = nc.dram_tensor("dbg_" + name, list(shape), ap_s.dtype, kind="ExternalOutput").ap()
            dbg_out[name] = d
            out_events.append(dma("sp", d, ap_s, toks, [], s_dbg))

        def load_slab(w_ap, row0, col0, nk, ncols=512):
            i = ring_i[0]
            ring_i[0] = (i + 1) % NSLOT
            cached = cur_tile[0] >= 0 and ncols == 512
            ct = CACHE_TILE.get(id(w_ap), 0) if cached else 0
            if cached:
                sid = slab_seq[0]
                slab_seq[0] += 1
            if cached and cur_tile[0] > ct:
                srcb = wscr[sid, :, 0:nk * 512].rearrange("p (k c) -> p k c", k=nk)
                dma("sp", ring[i][:, 0:nk, 0:512], srcb, [scr_tok[sid]], [ring_tok[i]], ring_sem[i])
                return ring[i], ring_tok[i]
            assert ncols == 512 and row0 % 1024 == 0 and col0 % 512 == 0
            widx = (col0 // 512) * NKH[id(w_ap)] + row0 // 1024
            src = w_ap[widx, :, 0:nk * 512].rearrange("p (k c) -> p k c", k=nk)
            dma("pool", ring[i][:, 0:nk, 0:ncols], src, [], [ring_tok[i]], ring_sem[i])
            if cached and cur_tile[0] == ct and ntiles > ct + 1:
                scr_tok[sid] = Tok(f"scr{sid}")
                dstb = wscr[sid, :, 0:nk * 512].rearrange("p (k c) -> p k c", k=nk)
                dma("sp", dstb, ring[i][:, 0:nk, 0:512], [ring_tok[i]], [scr_tok[sid]], st_sem[i])
            return ring[i], ring_tok[i]

        def fm_proj(w_ap, col0, row0, nkc, rhs_fn, rhs_toks, evac_fn, N=TT, ncols=512):
            nm = ncols // 128
            bs = [next_bank() for _ in range(nm)]
            for s0 in range(0, nkc, 8):
                nk = min(8, nkc - s0)
                slab, stok = load_slab(w_ap, row0 + s0 * 128, col0, nk, ncols)
                for m in range(nm):
                    for kk in range(nk):
                        k = s0 + kk
                        mm(banks[bs[m]][:, 0:N], slab[:, kk, m * 128:(m + 1) * 128], rhs_fn(k), k == 0, k == nkc - 1,
                           [stok] + rhs_toks, [bank_tok[bs[m]]])
                    if s0 + nk == nkc:
                        evac_fn(m, bs[m])

        def tm_proj(w_ap, col0, row0, nkc, lhs_fn, lhs_toks, evac_fn, nsub=NSUB, lhs_tok_fn=None):
            bs = [next_bank() for _ in range(nsub)]
            for s0 in range(0, nkc, 8):
                nk = min(8, nkc - s0)
                slab, stok = load_slab(w_ap, row0 + s0 * 128, col0, nk, 512)
                for su in range(nsub):
                    for kk in range(nk):
                        k = s0 + kk
                        mm(banks[bs[su]][:, :], lhs_fn(k, su), slab[:, kk, :], k == 0, k == nkc - 1,
                           [stok] + lhs_toks + (lhs_tok_fn(k) if lhs_tok_fn else []), [bank_tok[bs[su]]])
                    if s0 + nk == nkc:
                        evac_fn(su, bs[su])

        CACHE_TILE = {id(w_in): 0, id(w_branch): 0, id(w_out): 1, id(w_cq): 1, id(w_co): 1, id(w_down): 2, id(w_up): 1}

        dma("sp", params[:], params_d, [], [tk["params"]], s_m["params"])
        dma("sp", consts[:], consts_d, [], [tk["consts"]], s_m["consts"])
        dma("sp", wa2[:], wa2_d, [], [tk["wa2"]], s_m["wa2"])
        dma("pool", wa[:], wa_d.rearrange("(k p) c -> p k c", p=128), [], [tk["wa"]], s_m["wa"],
            allow_slow_non_contiguous=True)
        for g_ in range(4):
            dma("pool", wpool[:, g_, :, :], w_pool_d[g_].rearrange("(cc p) d -> p cc d", p=128), [], [tk["wpool"]], s_m["wpool"])
        cp("dve", ident[:], consts[:, CC_IDENT:CC_IDENT + 128], [tk["consts"]], [tk["ident"]])
        cp("dve", ones_bf[:], consts[:, CC_ONES:CC_ONES + 128], [tk["consts"]], [tk["ones_bf"]])
        P.op("pool", lambda e: e.memset(aT[:], 1.0), [], [tk["aT"]])
        P.op("pool", lambda e: e.memset(S[:], 0.0), [], [tk["S"]])
        P.op("pool", lambda e: e.memset(S_bf[:], 0.0), [], [tk["S_bf"]])
        P.op("pool", lambda e: e.memset(halo[:], 0.0), [], [tk["halo"]])
        P.op("pool", lambda e: e.memset(chalo[:], 0.0), [], [tk["chalo"]])
        P.op("dve", lambda e: e.memset(lutj[:], 1.0), [], [t_lutj])

        def lut_prefetch():
            act(lutj[:, 1:2], lutj[:, 0:1], AF.Ln, [t_lutj], [t_lutj])
        mask4 = consts[:, CC_MASK4:CC_MASK4 + 512]
        triu = consts[:, CC_TRIU:CC_TRIU + 128]
        ones_f = consts[:, CC_ONES:CC_ONES + 128]
        pcol = lambda c: params[:, c:c + 1]

        mq = ["sp"]
        gb_next = [None]

        def load_gb(row, region, off):
            t_gb = retok(region, ["gb"])
            gb = av(off, [D], F32)
            dma(mq[0], gb, gvec_d[row].partition_broadcast(128), [], [t_gb], s_gbr[region])
            return gb, t_gb

        def keep_warm(nj, b):
            for _ in range(nj):
                mm(banks[b][:, :], ident[:], hT[:, 0, :], True, True, [tk["ident"]], [bank_tok[b]])

        def norm_to_fm(src_fn, src_toks, nsub, gb, t_gb, dst, dst_tok, ncol, tmp_region="R3", tmp_off=A_R3, warm=0):
            t_junk, t_xn2 = retok(tmp_region, ["junk", "xn2"])
            junk = av(tmp_off, [D], BF16)
            xn2 = av(tmp_off + 4096, [D], BF16)
            xbufs = [(xn[:], tk["xn"]), (xn2, t_xn2)]
            def stats(su):
                src = src_fn(su)
                xb, t_xb = xbufs[su % 2]
                P.op("act", lambda e: e.activation(out=junk, in_=src, func=AF.Square, accum_out=ssq[:, su:su + 1]),
                     [src_toks[su]], [t_junk, tk_ssq[su]], nowaw=True)
                act(rstd[:, su:su + 1], ssq[:, su:su + 1], AF.Ln, [tk_ssq[su]], [tk_rstd[su]], bias=EPS, scale=1.0 / D)
                act(rstd[:, su:su + 1], rstd[:, su:su + 1], AF.Exp, [tk_rstd[su]], [tk_rstd[su]], scale=-0.5)
                stt("dve", xb, src, rstd[:, su:su + 1], gb, ALU.mult, ALU.mult, [src_toks[su], tk_rstd[su], t_gb], [t_xb])

            def transp(su):
                xb, t_xb = xbufs[su % 2]
                for half in range(2):
                    b = next_bank()
                    bb = banks_bf[b]
                    if su == 0 and half == 0 and warm:
                        keep_warm(warm, b)
                    for c8 in range(8):
                        c = half * 8 + c8
                        tr(bb[:, c8 * 128:(c8 + 1) * 128], xb[:, c * 128:(c + 1) * 128], [t_xb], [bank_tok[b]])
                    cp_any(dst[:, half * 8:(half + 1) * 8, su * 128:(su + 1) * 128],
                           bb[:, :].rearrange("p (a b) -> p a b", a=8), [bank_tok[b]], [dst_tok])

            stats(0)
            for su in range(nsub):
                if su + 1 < nsub:
                    stats(su + 1)
                transp(su)

        if stages >= 2:
            gb, t_gb = load_gb(2, "R1", A_R1)
            t_mem = retok("R2", ["memt"])
            memt = av(A_R2, [2, D], F32)
            dma("sp", memt, mem_d.rearrange("(s p) d -> p s d", p=128), [], [t_mem], s_m["memt"])
            t_mn = retok("R3", ["mem_nT"])
            mem_nT = av(A_R3, [KC, NMEM], BF16)
            norm_to_fm(lambda su: memt[:, su, :], [t_mem, t_mem], 2, gb, t_gb, mem_nT, t_mn, NMEM, "R5", A_R5)
            for n in range(4):
                def ev_k(m, b, n=n):
                    cp_any(KmT[:, n * 4 + m, :], banks[b][:, 0:NMEM], [bank_tok[b]], [tk["KmT"]])
                fm_proj(w_ckv, n * 512, 0, KC, lambda k: mem_nT[:, k, :], [t_mn], ev_k, N=NMEM)
            for n in range(4):
                def ev_v(su, b, n=n):
                    cp_any(Vm[:, su, n * 512:(n + 1) * 512], banks[b][:, :], [bank_tok[b]], [tk["Vm"]])
                tm_proj(w_ckv, D + n * 512, 0, KC, lambda k, su: mem_nT[:, k, su * 128:(su + 1) * 128], [t_mn], ev_v, nsub=2)

        for ti in range(ntiles):
            t0 = ti * TT
            cur_tile[0] = ti
            slab_seq[0] = 0
            mq[0] = "sp" if ti <= 2 else "pool"
            cur_stage[0] = 1
            if ti == 0:
                for su in range(NSUB):
                    dma(mq[0], xt[:, su, :], x_d[t0 + su * 128: t0 + (su + 1) * 128, :], [], [x_tok[su]], s_x[su])

            def resid_evac(n):
                def f(su, b):
                    tt("dve", xt[:, su, n * 512:(n + 1) * 512], xt[:, su, n * 512:(n + 1) * 512], banks[b][:, :], ALU.add,
                       [bank_tok[b], x_tok[su]], [x_tok[su]])
                return f

            gb, t_gb = gb_next[0] if gb_next[0] is not None else load_gb(0, "R6", A_R6)
            gb_next[0] = None
            norm_to_fm(lambda su: xt[:, su, :], x_tok, NSUB, gb, t_gb, hT, tk["hT"], TT, warm=0)
            hrhs = lambda k: hT[:, k, :]
            hlhs = lambda k, su: hT[:, k, su * 128:(su + 1) * 128]

            b = next_bank()
            for k in range(KC):
                mm(banks[b][0:RANK, :], wa[:, k, :], hT[:, k, :], k == 0, k == KC - 1, [tk["wa"], tk["hT"]], [bank_tok[b]])
            cp("dve", aT[0:RANK, :], banks[b][0:RANK, :], [bank_tok[b]], [tk["aT"]])
            t_sp, t_bT = retok("R1", ["sp", "bT"])
            sp = av(A_R1, [NSUB, DK], F32)
            bT = av(A_R1 + 8192, [NH, TT], F32)
            for su in range(NSUB):
                b = next_bank()
                mm(banks[b][:, :], aT[0:RANK + 1, su * 128:(su + 1) * 128], wa2[:, :], True, True, [tk["aT"], tk["wa2"]], [bank_tok[b]])
                act(sp[:, su, :], banks[b][:, :], AF.Exp, [bank_tok[b]], [t_sp], scale=-1.0)
            for su in range(NSUB):
                act(sp[:, su, :], sp[:, su, :], AF.Ln, [t_sp], [t_sp], bias=1.0)
            t_E = retok("R5", ["E1", "E2", "E3"])
            E1 = av(A_R5, [TT], F32)
            E2 = av(A_R5 + 2048, [TT], F32)
            E3 = av(A_R5 + 4096, [TT], F32)
            t_q, t_k, t_khT, t_kh = retok("R3", ["qT", "kT", "khT", "kh"])
            qT = av(A_R3, [NH, TT], BF16)
            kT = av(A_R3 + 4096, [NH, TT], BF16)
            khT = av(A_R3 + 8192, [NH, TT], BF16)
            kh = av(A_R3 + 12288, [NSUB, DK], BF16)
            qk_banks = {}

            def ev_hold(which):
                def f(m, b):
                    qk_banks[(which, m)] = b
                return f
            fm_proj(w_in, 0, 0, KC, hrhs, [tk["hT"]], ev_hold("q"))
            for h in range(NH):
                b = next_bank()
                for c in range(NSUB):
                    mm(banks[b][:, c * 128:(c + 1) * 128], sp[:, c, h * 128:(h + 1) * 128], triu, True, True,
                       [t_sp, tk["consts"]], [bank_tok[b]])
                cp("dve", bT[:, h, :], banks[b][:, :], [bank_tok[b]], [t_bT])
            for h in range(NH):
                act(E1, bT[:, h, :], AF.Exp, [t_bT], [t_E[0]])
                P.op("dve", lambda e, h=h: e.tensor_copy(out=dec[:, h, :], in_=E1.rearrange("p (c i) -> p c i", i=128)[:, :, 127]),
                     [t_E[0]], [tk["dec"]])
                bq = qk_banks[("q", h)]
                stt("dve", qT[:, h, :], banks[bq][:, :], float(HK) ** -0.5, E1, ALU.mult, ALU.mult, [bank_tok[bq], t_E[0]], [t_q])
            fm_proj(w_in, OFF_K, 0, KC, hrhs, [tk["hT"]], ev_hold("k"))
            for h in range(NH):
                act(E2, bT[:, h, :], AF.Exp, [t_bT], [t_E[1]], scale=-1.0)
                for c in range(NSUB):
                    act(E3[:, c * 128:(c + 1) * 128], bT[:, h, c * 128:(c + 1) * 128], AF.Exp, [t_bT], [t_E[2]],
                        scale=-1.0, bias=bT[:, h, c * 128 + 127:c * 128 + 128])
                bk = qk_banks[("k", h)]
                tt("dve", kT[:, h, :], banks[bk][:, :], E2, ALU.mult, [bank_tok[bk], t_E[1]], [t_k])
                tt("dve", khT[:, h, :], banks[bk][:, :], E3, ALU.mult, [bank_tok[bk], t_E[2]], [t_khT])
            t_v = retok("R4", ["v"])
            v = av(A_R4, [NSUB, DV], BF16)
            for n in range(2):
                def ev_v1(su, b, n=n):
                    cp_any(v[:, su, n * 512:(n + 1) * 512], banks[b][:, :], [bank_tok[b]], [t_v])
                tm_proj(w_in, OFF_V + n * 512, 0, KC, hlhs, [tk["hT"]], ev_v1)
            for c in range(NSUB):
                b = next_bank()
                for h in range(NH):
                    tr(banks_bf[b][:, h * 128:(h + 1) * 128], khT[:, h, c * 128:(c + 1) * 128], [t_khT], [bank_tok[b]])
                cp_any(kh[:, c, :], banks_bf[b][:, 0:DK], [bank_tok[b]], [t_kh])
            t_oT = retok("R2", ["oT"])
            oT = av(A_R2, [8, TT], F32)
            t_ATs = retok("R8", ["AT0", "AT1"])
            ATs = [av(A_R8 + i * 1024, [512], BF16) for i in range(2)]
            S_tok = [Tok(f"S{hp}") for hp in range(2)]
            Sb_tok = [Tok(f"Sb{hp}") for hp in range(2)]
            for hp in range(2):
                S_tok[hp] = tok_after(f"S{hp}", [tk["S"]])
                S_tok[hp].w = dict(tk["S"].w)
                Sb_tok[hp] = tok_after(f"Sb{hp}", [tk["S_bf"]])
                Sb_tok[hp].w = dict(tk["S_bf"].w)

            def scores_masked(c):
                cs_ = slice(c * 128, (c + 1) * 128)
                b = next_bank()
                for h in range(NH):
                    mm(banks[b][:, h * 128:(h + 1) * 128], kT[:, h, cs_], qT[:, h, cs_], True, True, [t_k, t_q], [bank_tok[b]])
                tt("dve", ATs[c % 2], banks[b][:, :], mask4, ALU.mult, [bank_tok[b], tk["consts"]], [t_ATs[c % 2]])

            scores_masked(0)
            for c in range(NSUB):
                cs = slice(c * 128, (c + 1) * 128)
                AT, t_AT = ATs[c % 2], t_ATs[c % 2]
                sb = []
                for hp in range(2):
                    b = next_bank()
                    sb.append(b)
                    for hh in range(2):
                        h = hp * 2 + hh
                        mm(banks[b][:, hh * HV:(hh + 1) * HV], kh[:, c, h * 128:(h + 1) * 128], v[:, c, h * HV:(h + 1) * HV], True, True,
                           [t_kh, t_v], [bank_tok[b]])
                for hp in range(2):
                    b = next_bank()
                    for hh in range(2):
                        h = hp * 2 + hh
                        for vc in range(2):
                            o_ps = banks[b][:, (hh * 2 + vc) * 128:(hh * 2 + vc + 1) * 128]
                            mm(o_ps, v[:, c, h * HV + vc * 128: h * HV + (vc + 1) * 128], AT[:, h * 128:(h + 1) * 128], True, False,
                               [t_v, t_AT], [bank_tok[b]])
                            mm(o_ps, S_bf[:, h, vc * 128:(vc + 1) * 128], qT[:, h, cs], False, True, [Sb_tok[hp], t_q], [bank_tok[b]])
                    cp_any(oT[:, hp * 4:(hp + 1) * 4, cs], banks[b][:, :].rearrange("p (a i) -> p a i", a=4), [bank_tok[b]], [t_oT])
                if c + 1 < NSUB:
                    scores_masked(c + 1)
                for hp in range(2):
                    b = sb[hp]
                    for hh in range(2):
                        h = hp * 2 + hh
                        stt("dve", S[:, h, :], S[:, h, :], dec[:, h, c:c + 1], banks[b][:, hh * HV:(hh + 1) * HV], ALU.mult, ALU.add,
                            [S_tok[hp], tk["dec"], bank_tok[b]], [S_tok[hp]])
                    cp("act", S_bf[:, hp * 2:hp * 2 + 2, :], S[:, hp * 2:hp * 2 + 2, :], [S_tok[hp]], [Sb_tok[hp]])
            tk["S"] = tok_after("S", [])
            tk["S_bf"] = tok_after("S_bf", [])
            for hp in range(2):
                for src_t, dst_t in ((S_tok[hp], tk["S"]), (Sb_tok[hp], tk["S_bf"])):
                    for kk_, ev in list(src_t.w.items()):
                        if kk_ not in dst_t.w or dst_t.w[kk_][1] < ev[1]:
                            dst_t.w[kk_] = ev
                    for kk_, ev in list(src_t.r.items()):
                        if kk_ not in dst_t.r or dst_t.r[kk_][1] < ev[1]:
                            dst_t.r[kk_] = ev
            t_sq0, t_sq1, t_ro, t_si, t_tm = retok("R1", ["sqh0", "sqh1", "rstd_o", "silu", "tmp"])
            sqhs = [(av(A_R1, [2, TT], BF16), t_sq0), (av(A_R1 + 2048, [2, TT], BF16), t_sq1)]
            rstd_o = av(A_R1 + 4096, [NH, TT], F32)
            silu_t = av(A_R1 + 12288, [TT], F32)
            tmp_t = av(A_R1 + 14336, [TT], F32)
            t_og = retok("R6", ["o_gla"])
            o_gla = av(A_R6, [8, TT], BF16)

            def ev_r_real(n, m, b):
                ch = n * 4 + m
                act(silu_t, banks[b][:, :], AF.Silu, [bank_tok[b]], [t_si])
                tt("dve", tmp_t, oT[:, ch, :], rstd_o[:, ch // 2, :], ALU.mult, [t_oT, t_ro], [t_tm])
                stt("dve", o_gla[:, ch, :], tmp_t, pcol(PC_GGLA + ch), silu_t, ALU.mult, ALU.mult, [t_tm, t_si, tk["params"]], [t_og])

            held = {}
            fm_proj(w_in, OFF_R, 0, KC, hrhs, [tk["hT"]], lambda m, b: held.__setitem__(m, b))
            def o_square(h):
                sqh, t_sq = sqhs[h % 2]
                act(sqh, oT[:, h * 2:(h + 1) * 2, :], AF.Square, [t_oT], [t_sq])

            o_square(0)
            o_square(1)
            for h in range(NH):
                sqh, t_sq = sqhs[h % 2]
                b = next_bank()
                for vc in range(2):
                    mm(banks[b][:, :], ones_bf[:], sqh[:, vc, :], vc == 0, vc == 1, [tk["ones_bf"], t_sq], [bank_tok[b]])
                act(rstd_o[:, h, :], banks[b][:, :], AF.Ln, [bank_tok[b]], [t_ro], bias=EPS, scale=1.0 / HV)
                act(rstd_o[:, h, :], rstd_o[:, h, :], AF.Exp, [t_ro], [t_ro], scale=-0.5)
                if h + 2 < NH:
                    o_square(h + 2)
            for m in range(4):
                ev_r_real(0, m, held[m])
            fm_proj(w_in, OFF_R + 512, 0, KC, hrhs, [tk["hT"]], lambda m, b: ev_r_real(1, m, b))
            t_P = retok("R1", ["P"])
            Pb = av(A_R1, [8, 528], F32)
            cp("dve", Pb[:, :, 0:16], halo[:], [tk["halo"]], [t_P])
            for n in range(2):
                def ev_p(m, b, n=n):
                    cp("act", Pb[:, n * 4 + m, 16:528], banks[b][:, :], [bank_tok[b]], [t_P])
                fm_proj(w_in, SOFF_P + n * 512, 0, KC, hrhs, [tk["hT"]], ev_p)
            cp("dve", halo[:], Pb[:, :, 512:528], [t_P], [tk["halo"]])
            t_wa, t_wb = retok("R5", ["tmpA", "tmpB"])
            wA = av(A_R5, [528], F32)
            wB = av(A_R5 + 2112, [528], F32)
            t_pl, t_op = retok("R3", ["pooled", "o_pool"])
            pooled = av(A_R3, [8, TT], BF16)
            o_pool = av(A_R3 + 8192, [8, TT], BF16)
            for ch in range(8):
                g_ = ch // 2
                w_ = 2 << g_
                src = Pb[:, ch, :]
                tt("dve", wA[:, 1:528], src[:, 1:528], src[:, 0:527], ALU.add, [t_P], [t_wa])
                cur, tcur = wA, t_wa
                if g_ >= 1:
                    tt("dve", wB[:, 3:528], wA[:, 3:528], wA[:, 1:526], ALU.add, [t_wa], [t_wb])
                    cur, tcur = wB, t_wb
                if g_ >= 2:
                    tt("dve", wA[:, 7:528], wB[:, 7:528], wB[:, 3:524], ALU.add, [t_wb], [t_wa])
                    cur, tcur = wA, t_wa
                if g_ >= 3:
                    tt("dve", wB[:, 15:528], wA[:, 15:528], wA[:, 7:520], ALU.add, [t_wa], [t_wb])
                    cur, tcur = wB, t_wb
                stt("dve", pooled[:, ch, :], cur[:, 16:528], 1.0 / w_, src[:, 16:528], ALU.mult, ALU.subtract, [tcur, t_P], [t_pl])
                if ti == 0:
                    oth, toth = (wA, t_wa) if cur is wB else (wB, t_wb)
                    tt("dve", oth[:, 0:16], cur[:, 16:32], consts[:, CC_INVC + g_ * 16: CC_INVC + (g_ + 1) * 16], ALU.mult,
                       [tcur, tk["consts"]], [toth])
                    tt("dve", pooled[:, ch, 0:16], oth[:, 0:16], src[:, 16:32], ALU.subtract, [toth, t_P], [t_pl])
            t_mg = retok("R2", ["mg0", "mg1", "mg2", "mg3"])
            merged = av(A_R2, [KC, TT], BF16)
            t_s1 = retok("R4", ["S1"])
            S1 = av(A_R4, [4, TT], F32)

            def merge_part1(n):
                def ev_g1(m, b):
                    act(S1[:, m, :], banks[b][:, :], AF.Sigmoid, [bank_tok[b]], [t_s1])
                fm_proj(w_in, SOFF_G + n * 512, 0, KC, hrhs, [tk["hT"]], ev_g1)

                def ev_yg(m, b):
                    tt("dve", S1[:, m, :], S1[:, m, :], banks[b][:, :], ALU.mult, [bank_tok[b], t_s1], [t_s1])
                fm_proj(w_branch, n * 512, 0, 8, lambda k: o_gla[:, k, :], [t_og], ev_yg)

            merge_part1(0)
            for g_ in range(4):
                for dc in range(2):
                    b = next_bank()
                    for cc in range(2):
                        mm(banks[b][:, :], wpool[:, g_, cc, dc * 128:(dc + 1) * 128], pooled[:, g_ * 2 + cc, :], cc == 0, cc == 1,
                           [tk["wpool"], t_pl], [bank_tok[b]])
                    act(o_pool[:, g_ * 2 + dc, :], banks[b][:, :], AF.Copy, [bank_tok[b], tk["params"]], [t_op],
                        scale=pcol(PC_PSCALE + g_ * 2 + dc))
            t_s2 = retok("R5", ["S2"])
            S2 = av(A_R5, [4, TT], F32)

            def merge_part2(n):
                def ev_g2(m, b):
                    act(S2[:, m, :], banks[b][:, :], AF.Sigmoid, [bank_tok[b]], [t_s2])
                fm_proj(w_in, SOFF_G + D + n * 512, 0, KC, hrhs, [tk["hT"]], ev_g2)

                def ev_yp(m, b):
                    tt("dve", S2[:, m, :], S2[:, m, :], banks[b][:, :], ALU.mult, [bank_tok[b], t_s2], [t_s2])
                    tt("dve", merged[:, n * 4 + m, :], S2[:, m, :], S1[:, m, :], ALU.add, [t_s1, t_s2], [t_mg[n]])
                fm_proj(w_branch, n * 512, DV, 8, lambda k: o_pool[:, k, :], [t_op], ev_yp)

            merge_part2(0)
            gb2 = load_gb(1, "R1", A_R1) if stages >= 2 else None
            for n in range(1, 4):
                merge_part1(n)
                merge_part2(n)
            lut_prefetch()
            for n in range(4):
                tm_proj(w_out, n * 512, 0, KC, lambda k, su: merged[:, k, su * 128:(su + 1) * 128], [], resid_evac(n),
                        lhs_tok_fn=lambda k: [t_mg[k // 4]])
            if "x1" in dbg and ti == 0:
                dump("x1", xt[:], x_tok, [128, NSUB, D])

            if stages >= 2:
                cur_stage[0] = 2
                gb, t_gb = gb2
                gb3 = load_gb(3, "R6", A_R6) if stages >= 3 else None
                norm_to_fm(lambda su: xt[:, su, :], x_tok, NSUB, gb, t_gb, hT, tk["hT"], TT, warm=0)
                t_qc = retok("R2", ["qcT"])
                qcT = av(A_R2, [KC, TT], BF16)
                for n in range(4):
                    def ev_q(m, b, n=n):
                        cp_any(qcT[:, n * 4 + m, :], banks[b][:, :], [bank_tok[b]], [t_qc])
                    fm_proj(w_cq, n * 512, 0, KC, hrhs, [tk["hT"]], ev_q)
                t_o2 = retok("R3", ["oT2_0", "oT2_1", "oT2_2", "oT2_3"])
                oT2 = av(A_R3, [KC, TT], BF16)
                t_pt = retok("R4", ["PT0", "PT1"])
                PTs = [av(A_R4 + i * 2048, [2, TT], BF16) for i in range(2)]
                t_rs = retok("R5", ["rs0", "rs1"])
                rss = [av(A_R5 + i * 2048, [TT], F32) for i in range(2)]

                def scores(h):
                    PT, tp = PTs[h % 2], t_pt[h % 2]
                    for mc in range(2):
                        b = next_bank()
                        for hc in range(4):
                            mm(banks[b][:, :], KmT[:, h * 4 + hc, mc * 128:(mc + 1) * 128], qcT[:, h * 4 + hc, :], hc == 0, hc == 3,
                               [tk["KmT"], t_qc], [bank_tok[b]])
                        act(PT[:, mc, :], banks[b][:, :], AF.Exp, [bank_tok[b]], [tp], scale=512.0 ** -0.5)

                def pv(h):
                    PT, tp = PTs[h % 2], t_pt[h % 2]
                    rs, tr_ = rss[h % 2], t_rs[h % 2]
                    b = next_bank()
                    for mc in range(2):
                        mm(banks[b][:, :], ones_bf[:], PT[:, mc, :], mc == 0, mc == 1, [tk["ones_bf"], tp], [bank_tok[b]])
                    act(rs, banks[b][:, :], AF.Ln, [bank_tok[b]], [tr_])
                    act(rs, rs, AF.Exp, [tr_], [tr_], scale=-1.0)
                    for hc in range(4):
                        b = next_bank()
                        for mc in range(2):
                            mm(banks[b][:, :], Vm[:, mc, h * 512 + hc * 128: h * 512 + (hc + 1) * 128], PT[:, mc, :], mc == 0, mc == 1,
                               [tk["Vm"], tp], [bank_tok[b]])
                        tt("dve", oT2[:, h * 4 + hc, :], banks[b][:, :], rs, ALU.mult, [bank_tok[b], tr_], [t_o2[h]])

                scores(0)
                for h in range(NH):
                    if h + 1 < NH:
                        scores(h + 1)
                    pv(h)
                for n in range(4):
                    tm_proj(w_co, n * 512, 0, KC, lambda k, su: oT2[:, k, su * 128:(su + 1) * 128], [], resid_evac(n),
                            lhs_tok_fn=lambda k: [t_o2[k // 4]])
                if "x2" in dbg and ti == 0:
                    dump("x2", xt[:], x_tok, [128, NSUB, D])

            if stages >= 3:
                cur_stage[0] = 3
                gb, t_gb = gb3
                norm_to_fm(lambda su: xt[:, su, :], x_tok, NSUB, gb, t_gb, hT, tk["hT"], TT, warm=0)
                olds_ = region_toks.get("R1", []) + region_toks.get("R2", []) + region_toks.get("R3", [])
                t_as = [tok_after(f"a3_{g}", olds_) for g in range(11)]
                region_toks["R1"] = list(t_as)
                region_toks["R2"] = list(t_as)
                region_toks["R3"] = list(t_as)
                a3 = av(A_R1, [NFC, TT], BF16)
                PCW = lambda c0: params[:, c0:c0 + 2 * NFC]
                h0 = chalo[:, :, 0]
                h1 = chalo[:, :, 1]
                rd = [tk["chalo"], tk["params"]]
                tt("dve", cb0[:], PCW(PC_CW1), h1, ALU.mult, rd, [tk["cb0"]])
                tt("dve", cb0[:], cb0[:], PCW(PC_CB), ALU.add, [tk["cb0"], tk["params"]], [tk["cb0"]])
                tt("dve", cb1[:], PCW(PC_CW0), h0, ALU.mult, rd, [tk["cb1"]])
                tt("dve", cb0[:], cb0[:], cb1[:], ALU.add, [tk["cb0"], tk["cb1"]], [tk["cb0"]])
                tt("dve", cb1[:], PCW(PC_CW0), h1, ALU.mult, rd + [tk["cb0"]], [tk["cb1"]])
                tt("dve", cb1[:], cb1[:], PCW(PC_CB), ALU.add, [tk["cb1"], tk["params"]], [tk["cb1"]])
                t_acc = retok("R6", ["accg0", "accg1", "accv0", "accv1"])
                accs = [av(A_R6 + i * 2048, [TT], F32) for i in range(4)]
                t_sg4 = retok("R4", ["sg0", "sg1", "sg2", "sg3"])
                sg4 = av(A_R4, [4, TT], F32)

                def conv(ai, b, fc):
                    acc, t_ac = accs[ai], t_acc[ai]
                    bk, tb = banks[b], bank_tok[b]
                    pr = [tb, tk["params"]]
                    act(acc[:, 2:TT], bk[:, 2:TT], AF.Identity, pr, [t_ac], scale=pcol(PC_CW2 + fc), bias=pcol(PC_CB + fc))
                    P.op("act", lambda e: e.activation(out=acc[:, 0:1], in_=bk[:, 0:1], func=AF.Identity, scale=pcol(PC_CW2 + fc),
                                                       bias=cb0[:, fc:fc + 1]), pr + [tk["cb0"]], [t_ac], nowaw=True)
                    P.op("act", lambda e: e.activation(out=acc[:, 1:2], in_=bk[:, 1:2], func=AF.Identity, scale=pcol(PC_CW2 + fc),
                                                       bias=cb1[:, fc:fc + 1]), pr + [tk["cb1"]], [t_ac], nowaw=True)
                    cp("act", chalo[:, fc, :], bk[:, TT - 2:TT], [tb], [tk["chalo"]], nowaw=True)
                    stt("dve", acc[:, 1:TT], bk[:, 0:TT - 1], pcol(PC_CW1 + fc), acc[:, 1:TT], ALU.mult, ALU.add, pr + [t_ac], [t_ac])
                    stt("dve", acc[:, 2:TT], bk[:, 0:TT - 2], pcol(PC_CW0 + fc), acc[:, 2:TT], ALU.mult, ALU.add, pr + [t_ac], [t_ac])

                for s_ in range(11):
                    def ev_gate(m, b, s_=s_):
                        fc = s_ * 4 + m
                        conv(m % 2, b, fc)
                        if m > 0:
                            act(sg4[:, m - 1, :], accs[(m - 1) % 2], AF.Silu, [t_acc[(m - 1) % 2]], [t_sg4[m - 1]])
                        if m == 3:
                            act(sg4[:, 3, :], accs[1], AF.Silu, [t_acc[1]], [t_sg4[3]])
                    fm_proj(w_up, s_ * 512, 0, KC, hrhs, [tk["hT"]], ev_gate)

                    def ev_val(m, b, s_=s_):
                        fc = s_ * 4 + m
                        conv(2 + m % 2, b, NFC + fc)
                        P.op("dve", lambda e: e.tensor_tensor(out=a3[:, fc, :], in0=sg4[:, m, :], in1=accs[2 + m % 2], op=ALU.mult),
                             [t_sg4[m], t_acc[2 + m % 2]], [t_as[s_]], nowaw=True)
                    fm_proj(w_up, DFF + s_ * 512, 0, KC, hrhs, [tk["hT"]], ev_val)
                for n in range(4):
                    tm_proj(w_down, n * 512, 0, NFC, lambda k, su: a3[:, k, su * 128:(su + 1) * 128], [], resid_evac(n),
                            lhs_tok_fn=lambda k: [t_as[k // 4]])
                    if n == 0:
                        lut_prefetch()
                        gbF = load_gb(4, "R4", A_R4)
                        if ti + 1 < ntiles:
                            gb_next[0] = load_gb(0, "R6", A_R6)

            gb, t_gb = gbF if stages >= 3 else load_gb(4, "R4", A_R4)
            olds_ = region_toks.get("R1", []) + region_toks.get("R2", [])
            t_sts = [tok_after(f"stage{su}", olds_) for su in range(NSUB)]
            region_toks["R1"] = list(t_sts)
            region_toks["R2"] = list(t_sts)
            stg = av(A_R1, [NSUB, D], F32)
            for su in range(NSUB):
                P.op("act", lambda e, su=su: e.activation(out=xn[:], in_=xt[:, su, :], func=AF.Square, accum_out=ssq[:, 4 + su:5 + su]),
                     [x_tok[su]], [tk["xn"], tk_ssq[4 + su]], nowaw=True)
                act(rstd[:, 4 + su:5 + su], ssq[:, 4 + su:5 + su], AF.Ln, [tk_ssq[4 + su]], [tk_rstd[4 + su]], bias=EPS, scale=1.0 / D)
                act(rstd[:, 4 + su:5 + su], rstd[:, 4 + su:5 + su], AF.Exp, [tk_rstd[4 + su]], [tk_rstd[4 + su]], scale=-0.5)
                P.op("dve", lambda e, su=su: e.scalar_tensor_tensor(out=stg[:, su, :], in0=xt[:, su, :], scalar=rstd[:, 4 + su:5 + su], in1=gb,
                                                                     op0=ALU.mult, op1=ALU.mult),
                     [x_tok[su], tk_rstd[4 + su], t_gb], [t_sts[su]])
                if ti + 1 < ntiles:
                    qn = "sp" if ti + 1 <= 2 else "pool"
                    dma(qn, xt[:, su, :], x_d[t0 + TT + su * 128: t0 + TT + (su + 1) * 128, :], [], [x_tok[su]], s_x[su])
                out_events.append(dma(mq[0], y_d[t0 + su * 128: t0 + (su + 1) * 128, :], stg[:, su, :], [t_sts[su]], [], s_out[su]))

        P.wait_events("sp", out_events)
        P.emit()
    return nc, dbg_out


def host_prep(inputs):
    f = lambda a: np.ascontiguousarray(np.asarray(a, dtype=np.float32))
    sq = lambda name: f(inputs[name])[0]
    col = lambda vec: vec.reshape(-1, 128).T
    params = np.zeros((128, PC_N), np.float32)
    params[:, PC_GMIX:PC_GMIX + 16] = col(sq("g_mix"))
    params[:, PC_GCROSS:PC_GCROSS + 16] = col(sq("g_cross"))
    params[:, PC_GMEM:PC_GMEM + 16] = col(sq("g_mem"))
    params[:, PC_GFFN:PC_GFFN + 16] = col(sq("g_ffn"))
    params[:, PC_GGLA:PC_GGLA + 8] = col(sq("g_gla"))
    params[:, PC_PSCALE:PC_PSCALE + 8] = col(sq("pool_scale"))
    cw = sq("conv_w")
    params[:, PC_CW0:PC_CW0 + 88] = col(cw[0])
    params[:, PC_CW1:PC_CW1 + 88] = col(cw[1])
    params[:, PC_CW2:PC_CW2 + 88] = col(cw[2])
    params[:, PC_CB:PC_CB + 88] = col(sq("conv_b"))
    consts = np.zeros((128, CC_N), np.float32)
    consts[:, CC_IDENT:CC_IDENT + 128] = np.eye(128, dtype=np.float32)
    j = np.arange(128)[:, None]
    i = np.arange(128)[None, :]
    m01 = (j <= i).astype(np.float32)
    consts[:, CC_MASK4:CC_MASK4 + 512] = np.tile(m01, (1, 4))
    consts[:, CC_TRIU:CC_TRIU + 128] = m01 * np.float32(-1.0 / 16.0)
    consts[:, CC_ONES:CC_ONES + 128] = 1.0
    for g_ in range(4):
        w_ = 2 << g_
        consts[:, CC_INVC + g_ * 16: CC_INVC + (g_ + 1) * 16] = 1.0 / np.minimum(np.arange(16) + 1, w_).astype(np.float32)
    def to_slabs(W):
        K, N = W.shape
        nkh = (K + 1023) // 1024
        if K != nkh * 1024:
            W = np.concatenate([W, np.zeros((nkh * 1024 - K, N), np.float32)], 0)
        W = W.reshape(nkh, 8, 128, N // 512, 512).transpose(3, 0, 2, 1, 4)
        return np.ascontiguousarray(W).reshape((N // 512) * nkh, 128, 8 * 512)

    win = sq("w_in")
    shared = {
        "w_in": to_slabs(np.concatenate([win[:, :OFF_A], win[:, OFF_P:]], 1)),
        "wa_in": np.ascontiguousarray(win[:, OFF_A:OFF_P]),
        "wa2": np.concatenate([sq("w_a2"), sq("b_a")[None, :]], 0),
        "w_pool": sq("w_pool"), "w_branch": to_slabs(sq("w_branch")), "w_out": to_slabs(sq("w_out")),
        "w_cq": to_slabs(sq("w_cq")), "w_ckv": to_slabs(sq("w_ckv")), "w_co": to_slabs(sq("w_co")),
        "w_up": to_slabs(sq("w_up")), "w_down": to_slabs(sq("w_down")),
        "gvec": np.stack([sq("g_mix"), sq("g_cross"), sq("g_mem"), sq("g_ffn"), f(inputs["g_final"])], 0),
        "params": params, "consts": consts,
    }
    return shared


def kernel(**inputs):
    shared = host_prep(inputs)
    x = np.asarray(inputs["x"], dtype=np.float32)
    mem = np.asarray(inputs["mem"], dtype=np.float32)
    nb = x.shape[0]
    nc, _ = build()
    in_maps = []
    for b in range(nb):
        m = dict(shared)
        m["x"] = np.ascontiguousarray(x[b])
        m["mem"] = np.ascontiguousarray(mem[b])
        in_maps.append(m)
    res = run_bass_kernel_spmd(nc, in_maps, core_ids=list(range(nb)))
    return np.stack([np.asarray(r["y"], dtype=np.float32) for r in res.results], 0)
```

```python
import numpy as np
from contextlib import ExitStack
import concourse.bass as bass
import concourse.mybir as mybir
from concourse.bass_utils import run_bass_kernel_spmd

F32 = mybir.dt.float32
BF16 = mybir.dt.bfloat16
AF = mybir.ActivationFunctionType
ALU = mybir.AluOpType

D = 2048
T = 2048
TT = 512
NSUB = 4
KC = 16
NH = 4
HK = 128
HV = 256
DK = 512
DV = 1024
RANK = 16
PW = 1024
DFF = 5632
NMEM = 256
EPS = 1e-6
OFF_K = 512
OFF_V = 1024
OFF_R = 2048
OFF_A = 3072
OFF_P = 3088
OFF_G = 4112
SOFF_P = 3072
SOFF_G = 4096
D_IN = 8208
NSLOT = 5
NFC = DFF // 128

PC_GMIX, PC_GCROSS, PC_GMEM, PC_GFFN = 0, 16, 32, 48
PC_GGLA = 64
PC_PSCALE = 72
PC_CW0 = 80
PC_CW1 = PC_CW0 + 88
PC_CW2 = PC_CW1 + 88
PC_CB = PC_CW2 + 88
PC_N = PC_CB + 88
CC_IDENT = 0
CC_MASK4 = 128
CC_TRIU = 128 + 512
CC_ONES = CC_TRIU + 128
CC_INVC = CC_ONES + 128
CC_N = CC_INVC + 64

A_R1 = 0
A_R2 = 16896
A_R3 = A_R2 + 16384
A_R4 = A_R3 + 16384
A_R5 = A_R4 + 8192
A_R6 = A_R5 + 8192
A_R8 = A_R6 + 8192
A_END = A_R8 + 2048


class Tok:
    __slots__ = ("w", "r", "name")

    def __init__(self, name=""):
        self.w = {}
        self.r = {}
        self.name = name


def tok_after(name, olds):
    t = Tok(name)
    for o in olds:
        evs = list(o.r.values()) + list(o.w.values())
        for ev in evs:
            k = id(ev[0])
            if k not in t.r or t.r[k][1] < ev[1]:
                t.r[k] = ev
    return t


class Prog:
    NOSELF = ("pe",)

    def __init__(self, nc, es):
        self.nc = nc
        self.es = es
        self.eng = {}
        for name in ("pe", "act", "dve", "pool", "sp"):
            sem = es.enter_context(nc.semaphore("c_" + name))
            self.eng[name] = dict(ops=[], n=0, known={}, sem=sem)
        self.dma_cnt = {}

    def dma_sem(self, name):
        s = self.es.enter_context(self.nc.semaphore(name))
        self.dma_cnt[id(s)] = [s, 0]
        return s

    def op(self, engine, fn, reads=(), writes=(), dma=None, nowaw=False):
        E = self.eng[engine]
        need = {}

        def add(ev):
            k = id(ev[0])
            if k not in need or need[k][1] < ev[1]:
                need[k] = ev

        for t in reads:
            for ev in t.w.values():
                add(ev)
        for t in writes:
            if not nowaw:
                for ev in t.w.values():
                    add(ev)
            for ev in t.r.values():
                add(ev)
        waits = []
        for k, (sem, val) in need.items():
            if sem is E["sem"] and engine in self.NOSELF:
                continue
            if E["known"].get(k, 0) >= val:
                continue
            E["known"][k] = val
            waits.append((sem, val))
        if dma is not None:
            c = self.dma_cnt[id(dma)]
            c[1] += 16
            ev = (dma, c[1])
            inc = (dma, 16)
        else:
            E["n"] += 1
            ev = (E["sem"], E["n"])
            inc = (E["sem"], 1)
        E["ops"].append((waits, fn, inc))
        k = id(ev[0])
        for t in reads:
            t.r[k] = ev
        for t in writes:
            if nowaw:
                t.w[k] = ev
            else:
                t.w = {k: ev}
                t.r = {}
        return ev

    def wait_events(self, engine, evs):
        E = self.eng[engine]
        waits = []
        for sem, val in evs:
            if E["known"].get(id(sem), 0) >= val:
                continue
            E["known"][id(sem)] = val
            waits.append((sem, val))
        E["ops"].append((waits, None, None))

    def emit(self):
        nc = self.nc
        with nc.Block() as block:
            def mk(name):
                def body(e):
                    for waits, fn, inc in self.eng[name]["ops"]:
                        for sem, val in waits:
                            e.wait_ge(sem, val)
                        if fn is not None:
                            ins = fn(e)
                            if inc is not None:
                                ins.then_inc(inc[0], inc[1])
                return body
            block.tensor(mk("pe"))
            block.scalar(mk("act"))
            block.vector(mk("dve"))
            block.gpsimd(mk("pool"))
            block.sync(mk("sp"))


def build(ntiles=4, stages=3, dbg=()):
    nc = bass.Bass("TRN2", target_bir_lowering=False)
    dram_in = lambda name, shape: nc.dram_tensor(name, list(shape), F32, kind="ExternalInput").ap()
    x_d = dram_in("x", [T, D])
    mem_d = dram_in("mem", [NMEM, D])
    NKH = {}

    def dram_w(name, K, N):
        nkh = (K + 1023) // 1024
        ap = dram_in(name, [(N // 512) * nkh, 128, 8 * 512])
        NKH[id(ap)] = nkh
        return ap
    w_in = dram_w("w_in", D, 8192)
    wa_d = dram_in("wa_in", [D, RANK])
    wa2_d = dram_in("wa2", [RANK + 1, DK])
    w_pool_d = dram_in("w_pool", [4, 256, 256])
    w_branch = dram_w("w_branch", D, D)
    w_out = dram_w("w_out", D, D)
    w_cq = dram_w("w_cq", D, D)
    w_ckv = dram_w("w_ckv", D, 2 * D)
    w_co = dram_w("w_co", D, D)
    w_up = dram_w("w_up", D, 2 * DFF)
    w_down = dram_w("w_down", DFF, D)
    gvec_d = dram_in("gvec", [5, D])
    params_d = dram_in("params", [128, PC_N])
    consts_d = dram_in("consts", [128, CC_N])
    y_d = nc.dram_tensor("y", [T, D], F32, kind="ExternalOutput").ap()
    dbg_out = {}

    with ExitStack() as es:
        P = Prog(nc, es)
        sb = lambda name, shape, dt: es.enter_context(nc.sbuf_tensor(name, list(shape), dt))

        xt = sb("xt", [128, NSUB, D], F32)
        hT = sb("hT", [128, KC, TT], BF16)
        ring = [sb(f"ring{i}", [128, 8, 512], BF16) for i in range(NSLOT)]
        xn = sb("xn", [128, D], BF16)
        KmT = sb("KmT", [128, KC, NMEM], BF16)
        Vm = sb("Vm", [128, 2, D], BF16)
        wpool = sb("wpool", [128, 4, 2, 256], BF16)
        S = sb("S", [128, NH, HV], F32)
        S_bf = sb("S_bf", [128, NH, HV], BF16)
        aT = sb("aT", [32, TT], F32)
        wa2 = sb("wa2s", [RANK + 1, DK], F32)
        wa = sb("wa", [128, KC, RANK], BF16)
        params = sb("params_s", [128, PC_N], F32)
        consts = sb("consts_s", [128, CC_N], F32)
        ident = sb("ident", [128, 128], BF16)
        ones_bf = sb("ones_bf", [128, 128], BF16)
        lutj = sb("lutj", [128, 2], F32)
        halo = sb("halo", [128, 8, 16], F32)
        chalo = sb("chalo", [128, 2 * NFC, 2], F32)
        dec = sb("dec", [128, NH, NSUB], F32)
        cb0 = sb("cb0", [128, 2 * NFC], F32)
        cb1 = sb("cb1", [128, 2 * NFC], F32)
        ssq = sb("ssq", [128, 8], F32)
        rstd = sb("rstd", [128, 8], F32)
        arena = sb("arena", [128, A_END // 2], BF16)
        arena32 = arena.bitcast(F32)

        banks = [es.enter_context(nc.psum_tensor(f"bank{i}", [128, 512], F32)) for i in range(8)]
        banks_bf = [b.bitcast(BF16) for b in banks]
        bank_tok = [Tok(f"bank{i}") for i in range(8)]
        bank_i = [0]

        def next_bank():
            i = bank_i[0]
            bank_i[0] = (i + 1) % 8
            return i

        def av(off, shape, dt):
            n = 1
            for s_ in shape:
                n *= s_
            if dt == BF16:
                assert off % 2 == 0
                ap = arena[:, off // 2: off // 2 + n]
            else:
                assert off % 4 == 0
                ap = arena32[:, off // 4: off // 4 + n]
            if len(shape) == 2:
                ap = ap.rearrange("p (a b) -> p a b", a=shape[0])
            elif len(shape) == 3:
                ap = ap.rearrange("p (a b c) -> p a b c", a=shape[0], b=shape[1])
            return ap

        region_toks = {}

        def retok(region, names):
            olds = region_toks.get(region, [])
            news = [tok_after(n, olds) for n in names]
            region_toks[region] = news
            return news if len(news) > 1 else news[0]

        tk = {n: Tok(n) for n in "hT xn KmT Vm wpool S S_bf aT wa2 wa params consts ident ones_bf halo chalo dec ssq rstd cb0 cb1".split()}
        x_tok = [Tok(f"x{s}") for s in range(NSUB)]
        tk_ssq = [Tok(f"ssq{s}") for s in range(8)]
        t_lutj = Tok("lutj")
        tk_rstd = [Tok(f"rstd{s}") for s in range(8)]
        ring_tok = [Tok(f"ring{i}") for i in range(NSLOT)]
        ring_sem = [P.dma_sem(f"s_ring{i}") for i in range(NSLOT)]
        ring_i = [0]
        st_sem = [P.dma_sem(f"s_st{i}") for i in range(NSLOT)]
        NSCR = 132
        wscr = nc.dram_tensor("wscr", [NSCR, 128, 8 * 512], BF16).ap()
        scr_tok = {}
        slab_seq = [0]
        cur_tile = [-1]
        cur_stage = [0]
        s_x = [P.dma_sem(f"s_x{s}") for s in range(NSUB)]
        s_m = {n: P.dma_sem("s_" + n) for n in ("params", "consts", "wa2", "wa", "wpool", "memt")}
        s_gbr = {r: P.dma_sem("s_gb" + r) for r in ("R1", "R4", "R6")}
        s_out = [P.dma_sem(f"s_out{s}") for s in range(NSUB)]
        s_dbg = P.dma_sem("s_dbg")
        out_events = []
        alt = [0]

        def act(out, in_, func, reads, writes, **kw):
            P.op("act", lambda e: e.activation(out=out, in_=in_, func=func, **kw), reads, writes)

        def tt(eng, out, in0, in1, op, reads, writes):
            P.op(eng, lambda e: e.tensor_tensor(out=out, in0=in0, in1=in1, op=op), reads, writes)

        def stt(eng, out, in0, scalar, in1, op0, op1, reads, writes):
            P.op(eng, lambda e: e.scalar_tensor_tensor(out=out, in0=in0, scalar=scalar, in1=in1, op0=op0, op1=op1), reads, writes)

        def ts(eng, out, in0, s1, s2, op0, op1, reads, writes):
            if s2 is None:
                P.op(eng, lambda e: e.tensor_scalar(out=out, in0=in0, scalar1=s1, scalar2=None, op0=op0), reads, writes)
            else:
                P.op(eng, lambda e: e.tensor_scalar(out=out, in0=in0, scalar1=s1, scalar2=s2, op0=op0, op1=op1), reads, writes)

        def cp(eng, out, in_, reads, writes, nowaw=False):
            if eng == "act":
                P.op("act", lambda e: e.activation(out=out, in_=in_, func=AF.Copy), reads, writes, nowaw=nowaw)
            else:
                P.op(eng, lambda e: e.tensor_copy(out=out, in_=in_), reads, writes, nowaw=nowaw)

        def cp_any(out, in_, reads, writes):
            alt[0] ^= 1
            cp("act" if alt[0] else "dve", out, in_, reads, writes, nowaw=True)

        def mm(out, lhsT, rhs, start, stop, reads, writes):
            P.op("pe", lambda e: e.matmul(out, lhsT=lhsT, rhs=rhs, start=start, stop=stop), reads, writes)

        def tr(out, in_, reads, writes):
            P.op("pe", lambda e: e.transpose(out=out, in_=in_, identity=ident[:]), list(reads) + [tk["ident"]], writes)

        def dma(eng, out, in_, reads, writes, sem, **kw):
            return P.op(eng, lambda e: e.dma_start(out=out, in_=in_, **kw), reads, writes, dma=sem)

        def dump(name, ap_s, toks, shape):
            d = nc.dram_tensor("dbg_" + name, list(shape), ap_s.dtype, kind="ExternalOutput").ap()
            dbg_out[name] = d
            out_events.append(dma("sp", d, ap_s, toks, [], s_dbg))

        def load_slab(w_ap, row0, col0, nk, ncols=512):
            i = ring_i[0]
            ring_i[0] = (i + 1) % NSLOT
            cached = cur_tile[0] >= 0 and ncols == 512
            ct = CACHE_TILE.get(id(w_ap), 0) if cached else 0
            if cached:
                sid = slab_seq[0]
                slab_seq[0] += 1
            if cached and cur_tile[0] > ct:
                srcb = wscr[sid, :, 0:nk * 512].rearrange("p (k c) -> p k c", k=nk)
                dma("sp", ring[i][:, 0:nk, 0:512], srcb, [scr_tok[sid]], [ring_tok[i]], ring_sem[i])
                return ring[i], ring_tok[i]
            assert ncols == 512 and row0 % 1024 == 0 and col0 % 512 == 0
            widx = (col0 // 512) * NKH[id(w_ap)] + row0 // 1024
            src = w_ap[widx, :, 0:nk * 512].rearrange("p (k c) -> p k c", k=nk)
            dma("pool", ring[i][:, 0:nk, 0:ncols], src, [], [ring_tok[i]], ring_sem[i])
            if cached and cur_tile[0] == ct and ntiles > ct + 1:
                scr_tok[sid] = Tok(f"scr{sid}")
                dstb = wscr[sid, :, 0:nk * 512].rearrange("p (k c) -> p k c", k=nk)
                dma("sp", dstb, ring[i][:, 0:nk, 0:512], [ring_tok[i]], [scr_tok[sid]], st_sem[i])
            return ring[i], ring_tok[i]

        def fm_proj(w_ap, col0, row0, nkc, rhs_fn, rhs_toks, evac_fn, N=TT, ncols=512):
            nm = ncols // 128
            bs = [next_bank() for _ in range(nm)]
            for s0 in range(0, nkc, 8):
                nk = min(8, nkc - s0)
                slab, stok = load_slab(w_ap, row0 + s0 * 128, col0, nk, ncols)
                for m in range(nm):
                    for kk in range(nk):
                        k = s0 + kk
                        mm(banks[bs[m]][:, 0:N], slab[:, kk, m * 128:(m + 1) * 128], rhs_fn(k), k == 0, k == nkc - 1,
                           [stok] + rhs_toks, [bank_tok[bs[m]]])
                    if s0 + nk == nkc:
                        evac_fn(m, bs[m])

        def tm_proj(w_ap, col0, row0, nkc, lhs_fn, lhs_toks, evac_fn, nsub=NSUB, lhs_tok_fn=None):
            bs = [next_bank() for _ in range(nsub)]
            for s0 in range(0, nkc, 8):
                nk = min(8, nkc - s0)
                slab, stok = load_slab(w_ap, row0 + s0 * 128, col0, nk, 512)
                for su in range(nsub):
                    for kk in range(nk):
                        k = s0 + kk
                        mm(banks[bs[su]][:, :], lhs_fn(k, su), slab[:, kk, :], k == 0, k == nkc - 1,
                           [stok] + lhs_toks + (lhs_tok_fn(k) if lhs_tok_fn else []), [bank_tok[bs[su]]])
                    if s0 + nk == nkc:
                        evac_fn(su, bs[su])

        CACHE_TILE = {id(w_in): 0, id(w_branch): 0, id(w_out): 1, id(w_cq): 1, id(w_co): 1, id(w_down): 2, id(w_up): 1}

        dma("sp", params[:], params_d, [], [tk["params"]], s_m["params"])
        dma("sp", consts[:], consts_d, [], [tk["consts"]], s_m["consts"])
        dma("sp", wa2[:], wa2_d, [], [tk["wa2"]], s_m["wa2"])
        cp("dve", ident[:], consts[:, CC_IDENT:CC_IDENT + 128], [tk["consts"]], [tk["ident"]])
        cp("dve", ones_bf[:], consts[:, CC_ONES:CC_ONES + 128], [tk["consts"]], [tk["ones_bf"]])
        P.op("dve", lambda e: e.memset(lutj[:], 1.0), [], [t_lutj])

        def lut_prefetch():
            act(lutj[:, 1:2], lutj[:, 0:1], AF.Ln, [t_lutj], [t_lutj])
        mask4 = consts[:, CC_MASK4:CC_MASK4 + 512]
        triu = consts[:, CC_TRIU:CC_TRIU + 128]
        ones_f = consts[:, CC_ONES:CC_ONES + 128]
        pcol = lambda c: params[:, c:c + 1]

        mq = ["sp"]
        gb_next = [None]

        def load_gb(row, region, off):
            t_gb = retok(region, ["gb"])
            gb = av(off, [D], F32)
            dma(mq[0], gb, gvec_d[row].partition_broadcast(128), [], [t_gb], s_gbr[region])
            return gb, t_gb

        def keep_warm(nj, b):
            for _ in range(nj):
                mm(banks[b][:, :], ident[:], hT[:, 0, :], True, True, [tk["ident"]], [bank_tok[b]])

        def norm_to_fm(src_fn, src_toks, nsub, gb, t_gb, dst, dst_tok, ncol, tmp_region="R3", tmp_off=A_R3, warm=0):
            t_junk, t_xn2 = retok(tmp_region, ["junk", "xn2"])
            junk = av(tmp_off, [D], BF16)
            xn2 = av(tmp_off + 4096, [D], BF16)
            xbufs = [(xn[:], tk["xn"]), (xn2, t_xn2)]
            def stats(su):
                src = src_fn(su)
                xb, t_xb = xbufs[su % 2]
                P.op("act", lambda e: e.activation(out=junk, in_=src, func=AF.Square, accum_out=ssq[:, su:su + 1]),
                     [src_toks[su]], [t_junk, tk_ssq[su]], nowaw=True)
                act(rstd[:, su:su + 1], ssq[:, su:su + 1], AF.Ln, [tk_ssq[su]], [tk_rstd[su]], bias=EPS, scale=1.0 / D)
                act(rstd[:, su:su + 1], rstd[:, su:su + 1], AF.Exp, [tk_rstd[su]], [tk_rstd[su]], scale=-0.5)
                stt("dve", xb, src, rstd[:, su:su + 1], gb, ALU.mult, ALU.mult, [src_toks[su], tk_rstd[su], t_gb], [t_xb])

            def transp(su):
                xb, t_xb = xbufs[su % 2]
                for half in range(2):
                    b = next_bank()
                    bb = banks_bf[b]
                    if su == 0 and half == 0 and warm:
                        keep_warm(warm, b)
                    for c8 in range(8):
                        c = half * 8 + c8
                        tr(bb[:, c8 * 128:(c8 + 1) * 128], xb[:, c * 128:(c + 1) * 128], [t_xb], [bank_tok[b]])
                    cp_any(dst[:, half * 8:(half + 1) * 8, su * 128:(su + 1) * 128],
                           bb[:, :].rearrange("p (a b) -> p a b", a=8), [bank_tok[b]], [dst_tok])

            stats(0)
            for su in range(nsub):
                if su + 1 < nsub:
                    stats(su + 1)
                transp(su)

        if stages >= 2:
            gb, t_gb = load_gb(2, "R1", A_R1)
            t_mem = retok("R2", ["memt"])
            memt = av(A_R2, [2, D], F32)
            dma("sp", memt, mem_d.rearrange("(s p) d -> p s d", p=128), [], [t_mem], s_m["memt"])
            t_mn = retok("R3", ["mem_nT"])
            mem_nT = av(A_R3, [KC, NMEM], BF16)
            norm_to_fm(lambda su: memt[:, su, :], [t_mem, t_mem], 2, gb, t_gb, mem_nT, t_mn, NMEM, "R5", A_R5)
            for n in range(4):
                def ev_k(m, b, n=n):
                    cp_any(KmT[:, n * 4 + m, :], banks[b][:, 0:NMEM], [bank_tok[b]], [tk["KmT"]])
                fm_proj(w_ckv, n * 512, 0, KC, lambda k: mem_nT[:, k, :], [t_mn], ev_k, N=NMEM)
            for n in range(4):
                def ev_v(su, b, n=n):
                    cp_any(Vm[:, su, n * 512:(n + 1) * 512], banks[b][:, :], [bank_tok[b]], [tk["Vm"]])
                tm_proj(w_ckv, D + n * 512, 0, KC, lambda k, su: mem_nT[:, k, su * 128:(su + 1) * 128], [t_mn], ev_v, nsub=2)

        dma("pool", wa[:], wa_d.rearrange("(k p) c -> p k c", p=128), [], [tk["wa"]], s_m["wa"],
            allow_slow_non_contiguous=True)
        for g_ in range(4):
            dma("pool", wpool[:, g_, :, :], w_pool_d[g_].rearrange("(cc p) d -> p cc d", p=128), [], [tk["wpool"]], s_m["wpool"])
        P.op("pool", lambda e: e.memset(aT[:], 1.0), [], [tk["aT"]])
        P.op("pool", lambda e: e.memset(S[:], 0.0), [], [tk["S"]])
        P.op("pool", lambda e: e.memset(S_bf[:], 0.0), [], [tk["S_bf"]])
        P.op("pool", lambda e: e.memset(halo[:], 0.0), [], [tk["halo"]])
        P.op("pool", lambda e: e.memset(chalo[:], 0.0), [], [tk["chalo"]])
        for ti in range(ntiles):
            t0 = ti * TT
            cur_tile[0] = ti
            slab_seq[0] = 0
            mq[0] = "sp" if ti <= 2 else "pool"
            cur_stage[0] = 1
            if ti == 0:
                for su in range(NSUB):
                    dma(mq[0], xt[:, su, :], x_d[t0 + su * 128: t0 + (su + 1) * 128, :], [], [x_tok[su]], s_x[su])

            def resid_evac(n):
                def f(su, b):
                    tt("dve", xt[:, su, n * 512:(n + 1) * 512], xt[:, su, n * 512:(n + 1) * 512], banks[b][:, :], ALU.add,
                       [bank_tok[b], x_tok[su]], [x_tok[su]])
                return f

            gb, t_gb = gb_next[0] if gb_next[0] is not None else load_gb(0, "R6", A_R6)
            gb_next[0] = None
            norm_to_fm(lambda su: xt[:, su, :], x_tok, NSUB, gb, t_gb, hT, tk["hT"], TT, warm=0)
            hrhs = lambda k: hT[:, k, :]
            hlhs = lambda k, su: hT[:, k, su * 128:(su + 1) * 128]

            b = next_bank()
            for k in range(KC):
                mm(banks[b][0:RANK, :], wa[:, k, :], hT[:, k, :], k == 0, k == KC - 1, [tk["wa"], tk["hT"]], [bank_tok[b]])
            cp("dve", aT[0:RANK, :], banks[b][0:RANK, :], [bank_tok[b]], [tk["aT"]])
            t_sp, t_bT = retok("R1", ["sp", "bT"])
            sp = av(A_R1, [NSUB, DK], F32)
            bT = av(A_R1 + 8192, [NH, TT], F32)
            for su in range(NSUB):
                b = next_bank()
                mm(banks[b][:, :], aT[0:RANK + 1, su * 128:(su + 1) * 128], wa2[:, :], True, True, [tk["aT"], tk["wa2"]], [bank_tok[b]])
                act(sp[:, su, :], banks[b][:, :], AF.Exp, [bank_tok[b]], [t_sp], scale=-1.0)
            for su in range(NSUB):
                act(sp[:, su, :], sp[:, su, :], AF.Ln, [t_sp], [t_sp], bias=1.0)
            t_E = retok("R5", ["E1", "E2", "E3"])
            E1 = av(A_R5, [TT], F32)
            E2 = av(A_R5 + 2048, [TT], F32)
            E3 = av(A_R5 + 4096, [TT], F32)
            t_q, t_k, t_khT, t_kh = retok("R3", ["qT", "kT", "khT", "kh"])
            qT = av(A_R3, [NH, TT], BF16)
            kT = av(A_R3 + 4096, [NH, TT], BF16)
            khT = av(A_R3 + 8192, [NH, TT], BF16)
            kh = av(A_R3 + 12288, [NSUB, DK], BF16)
            qk_banks = {}

            def ev_hold(which):
                def f(m, b):
                    qk_banks[(which, m)] = b
                return f
            fm_proj(w_in, 0, 0, KC, hrhs, [tk["hT"]], ev_hold("q"))
            for h in range(NH):
                b = next_bank()
                for c in range(NSUB):
                    mm(banks[b][:, c * 128:(c + 1) * 128], sp[:, c, h * 128:(h + 1) * 128], triu, True, True,
                       [t_sp, tk["consts"]], [bank_tok[b]])
                cp("dve", bT[:, h, :], banks[b][:, :], [bank_tok[b]], [t_bT])
            for h in range(NH):
                act(E1, bT[:, h, :], AF.Exp, [t_bT], [t_E[0]])
                P.op("dve", lambda e, h=h: e.tensor_copy(out=dec[:, h, :], in_=E1.rearrange("p (c i) -> p c i", i=128)[:, :, 127]),
                     [t_E[0]], [tk["dec"]])
                bq = qk_banks[("q", h)]
                stt("dve", qT[:, h, :], banks[bq][:, :], float(HK) ** -0.5, E1, ALU.mult, ALU.mult, [bank_tok[bq], t_E[0]], [t_q])
            fm_proj(w_in, OFF_K, 0, KC, hrhs, [tk["hT"]], ev_hold("k"))
            for h in range(NH):
                act(E2, bT[:, h, :], AF.Exp, [t_bT], [t_E[1]], scale=-1.0)
                for c in range(NSUB):
                    act(E3[:, c * 128:(c + 1) * 128], bT[:, h, c * 128:(c + 1) * 128], AF.Exp, [t_bT], [t_E[2]],
                        scale=-1.0, bias=bT[:, h, c * 128 + 127:c * 128 + 128])
                bk = qk_banks[("k", h)]
                tt("dve", kT[:, h, :], banks[bk][:, :], E2, ALU.mult, [bank_tok[bk], t_E[1]], [t_k])
                tt("dve", khT[:, h, :], banks[bk][:, :], E3, ALU.mult, [bank_tok[bk], t_E[2]], [t_khT])
            t_v = retok("R4", ["v"])
            v = av(A_R4, [NSUB, DV], BF16)
            for n in range(2):
                def ev_v1(su, b, n=n):
                    cp_any(v[:, su, n * 512:(n + 1) * 512], banks[b][:, :], [bank_tok[b]], [t_v])
                tm_proj(w_in, OFF_V + n * 512, 0, KC, hlhs, [tk["hT"]], ev_v1)
            for c in range(NSUB):
                b = next_bank()
                for h in range(NH):
                    tr(banks_bf[b][:, h * 128:(h + 1) * 128], khT[:, h, c * 128:(c + 1) * 128], [t_khT], [bank_tok[b]])
                cp_any(kh[:, c, :], banks_bf[b][:, 0:DK], [bank_tok[b]], [t_kh])
            t_oT = retok("R2", ["oT"])
            oT = av(A_R2, [8, TT], F32)
            t_ATs = retok("R8", ["AT0", "AT1"])
            ATs = [av(A_R8 + i * 1024, [512], BF16) for i in range(2)]
            S_tok = [Tok(f"S{hp}") for hp in range(2)]
            Sb_tok = [Tok(f"Sb{hp}") for hp in range(2)]
            for hp in range(2):
                S_tok[hp] = tok_after(f"S{hp}", [tk["S"]])
                S_tok[hp].w = dict(tk["S"].w)
                Sb_tok[hp] = tok_after(f"Sb{hp}", [tk["S_bf"]])
                Sb_tok[hp].w = dict(tk["S_bf"].w)

            def scores_masked(c):
                cs_ = slice(c * 128, (c + 1) * 128)
                b = next_bank()
                for h in range(NH):
                    mm(banks[b][:, h * 128:(h + 1) * 128], kT[:, h, cs_], qT[:, h, cs_], True, True, [t_k, t_q], [bank_tok[b]])
                tt("dve", ATs[c % 2], banks[b][:, :], mask4, ALU.mult, [bank_tok[b], tk["consts"]], [t_ATs[c % 2]])

            scores_masked(0)
            for c in range(NSUB):
                cs = slice(c * 128, (c + 1) * 128)
                AT, t_AT = ATs[c % 2], t_ATs[c % 2]
                sb = []
                for hp in range(2):
                    b = next_bank()
                    sb.append(b)
                    for hh in range(2):
                        h = hp * 2 + hh
                        mm(banks[b][:, hh * HV:(hh + 1) * HV], kh[:, c, h * 128:(h + 1) * 128], v[:, c, h * HV:(h + 1) * HV], True, True,
                           [t_kh, t_v], [bank_tok[b]])
                for hp in range(2):
                    b = next_bank()
                    for hh in range(2):
                        h = hp * 2 + hh
                        for vc in range(2):
                            o_ps = banks[b][:, (hh * 2 + vc) * 128:(hh * 2 + vc + 1) * 128]
                            mm(o_ps, v[:, c, h * HV + vc * 128: h * HV + (vc + 1) * 128], AT[:, h * 128:(h + 1) * 128], True, False,
                               [t_v, t_AT], [bank_tok[b]])
                            mm(o_ps, S_bf[:, h, vc * 128:(vc + 1) * 128], qT[:, h, cs], False, True, [Sb_tok[hp], t_q], [bank_tok[b]])
                    cp_any(oT[:, hp * 4:(hp + 1) * 4, cs], banks[b][:, :].rearrange("p (a i) -> p a i", a=4), [bank_tok[b]], [t_oT])
                if c + 1 < NSUB:
                    scores_masked(c + 1)
                for hp in range(2):
                    b = sb[hp]
                    for hh in range(2):
                        h = hp * 2 + hh
                        stt("dve", S[:, h, :], S[:, h, :], dec[:, h, c:c + 1], banks[b][:, hh * HV:(hh + 1) * HV], ALU.mult, ALU.add,
                            [S_tok[hp], tk["dec"], bank_tok[b]], [S_tok[hp]])
                    cp("act", S_bf[:, hp * 2:hp * 2 + 2, :], S[:, hp * 2:hp * 2 + 2, :], [S_tok[hp]], [Sb_tok[hp]])
            tk["S"] = tok_after("S", [])
            tk["S_bf"] = tok_after("S_bf", [])
            for hp in range(2):
                for src_t, dst_t in ((S_tok[hp], tk["S"]), (Sb_tok[hp], tk["S_bf"])):
                    for kk_, ev in list(src_t.w.items()):
                        if kk_ not in dst_t.w or dst_t.w[kk_][1] < ev[1]:
                            dst_t.w[kk_] = ev
                    for kk_, ev in list(src_t.r.items()):
                        if kk_ not in dst_t.r or dst_t.r[kk_][1] < ev[1]:
                            dst_t.r[kk_] = ev
            t_sq0, t_sq1, t_ro, t_si, t_tm = retok("R1", ["sqh0", "sqh1", "rstd_o", "silu", "tmp"])
            sqhs = [(av(A_R1, [2, TT], BF16), t_sq0), (av(A_R1 + 2048, [2, TT], BF16), t_sq1)]
            rstd_o = av(A_R1 + 4096, [NH, TT], F32)
            silu_t = av(A_R1 + 12288, [TT], F32)
            tmp_t = av(A_R1 + 14336, [TT], F32)
            t_og = retok("R6", ["o_gla"])
            o_gla = av(A_R6, [8, TT], BF16)

            def ev_r_real(n, m, b):
                ch = n * 4 + m
                act(silu_t, banks[b][:, :], AF.Silu, [bank_tok[b]], [t_si])
                tt("dve", tmp_t, oT[:, ch, :], rstd_o[:, ch // 2, :], ALU.mult, [t_oT, t_ro], [t_tm])
                stt("dve", o_gla[:, ch, :], tmp_t, pcol(PC_GGLA + ch), silu_t, ALU.mult, ALU.mult, [t_tm, t_si, tk["params"]], [t_og])

            held = {}
            fm_proj(w_in, OFF_R, 0, KC, hrhs, [tk["hT"]], lambda m, b: held.__setitem__(m, b))
            def o_square(h):
                sqh, t_sq = sqhs[h % 2]
                act(sqh, oT[:, h * 2:(h + 1) * 2, :], AF.Square, [t_oT], [t_sq])

            o_square(0)
            o_square(1)
            for h in range(NH):
                sqh, t_sq = sqhs[h % 2]
                b = next_bank()
                for vc in range(2):
                    mm(banks[b][:, :], ones_bf[:], sqh[:, vc, :], vc == 0, vc == 1, [tk["ones_bf"], t_sq], [bank_tok[b]])
                act(rstd_o[:, h, :], banks[b][:, :], AF.Ln, [bank_tok[b]], [t_ro], bias=EPS, scale=1.0 / HV)
                act(rstd_o[:, h, :], rstd_o[:, h, :], AF.Exp, [t_ro], [t_ro], scale=-0.5)
                if h + 2 < NH:
                    o_square(h + 2)
            for m in range(4):
                ev_r_real(0, m, held[m])
            fm_proj(w_in, OFF_R + 512, 0, KC, hrhs, [tk["hT"]], lambda m, b: ev_r_real(1, m, b))
            t_P = retok("R1", ["P"])
            Pb = av(A_R1, [8, 528], F32)
            cp("dve", Pb[:, :, 0:16], halo[:], [tk["halo"]], [t_P])
            for n in range(2):
                def ev_p(m, b, n=n):
                    cp("act", Pb[:, n * 4 + m, 16:528], banks[b][:, :], [bank_tok[b]], [t_P])
                fm_proj(w_in, SOFF_P + n * 512, 0, KC, hrhs, [tk["hT"]], ev_p)
            cp("dve", halo[:], Pb[:, :, 512:528], [t_P], [tk["halo"]])
            t_wa, t_wb = retok("R5", ["tmpA", "tmpB"])
            wA = av(A_R5, [528], F32)
            wB = av(A_R5 + 2112, [528], F32)
            t_pl, t_op = retok("R3", ["pooled", "o_pool"])
            pooled = av(A_R3, [8, TT], BF16)
            o_pool = av(A_R3 + 8192, [8, TT], BF16)
            for ch in range(8):
                g_ = ch // 2
                w_ = 2 << g_
                src = Pb[:, ch, :]
                tt("dve", wA[:, 1:528], src[:, 1:528], src[:, 0:527], ALU.add, [t_P], [t_wa])
                cur, tcur = wA, t_wa
                if g_ >= 1:
                    tt("dve", wB[:, 3:528], wA[:, 3:528], wA[:, 1:526], ALU.add, [t_wa], [t_wb])
                    cur, tcur = wB, t_wb
                if g_ >= 2:
                    tt("dve", wA[:, 7:528], wB[:, 7:528], wB[:, 3:524], ALU.add, [t_wb], [t_wa])
                    cur, tcur = wA, t_wa
                if g_ >= 3:
                    tt("dve", wB[:, 15:528], wA[:, 15:528], wA[:, 7:520], ALU.add, [t_wa], [t_wb])
                    cur, tcur = wB, t_wb
                stt("dve", pooled[:, ch, :], cur[:, 16:528], 1.0 / w_, src[:, 16:528], ALU.mult, ALU.subtract, [tcur, t_P], [t_pl])
                if ti == 0:
                    oth, toth = (wA, t_wa) if cur is wB else (wB, t_wb)
                    tt("dve", oth[:, 0:16], cur[:, 16:32], consts[:, CC_INVC + g_ * 16: CC_INVC + (g_ + 1) * 16], ALU.mult,
                       [tcur, tk["consts"]], [toth])
                    tt("dve", pooled[:, ch, 0:16], oth[:, 0:16], src[:, 16:32], ALU.subtract, [toth, t_P], [t_pl])
            t_mg = retok("R2", ["mg0", "mg1", "mg2", "mg3"])
            merged = av(A_R2, [KC, TT], BF16)
            t_s1 = retok("R4", ["S1"])
            S1 = av(A_R4, [4, TT], F32)

            def merge_part1(n):
                def ev_g1(m, b):
                    act(S1[:, m, :], banks[b][:, :], AF.Sigmoid, [bank_tok[b]], [t_s1])
                fm_proj(w_in, SOFF_G + n * 512, 0, KC, hrhs, [tk["hT"]], ev_g1)

                def ev_yg(m, b):
                    tt("dve", S1[:, m, :], S1[:, m, :], banks[b][:, :], ALU.mult, [bank_tok[b], t_s1], [t_s1])
                fm_proj(w_branch, n * 512, 0, 8, lambda k: o_gla[:, k, :], [t_og], ev_yg)

            merge_part1(0)
            for g_ in range(4):
                for dc in range(2):
                    b = next_bank()
                    for cc in range(2):
                        mm(banks[b][:, :], wpool[:, g_, cc, dc * 128:(dc + 1) * 128], pooled[:, g_ * 2 + cc, :], cc == 0, cc == 1,
                           [tk["wpool"], t_pl], [bank_tok[b]])
                    act(o_pool[:, g_ * 2 + dc, :], banks[b][:, :], AF.Copy, [bank_tok[b], tk["params"]], [t_op],
                        scale=pcol(PC_PSCALE + g_ * 2 + dc))
            t_s2 = retok("R5", ["S2"])
            S2 = av(A_R5, [4, TT], F32)

            def merge_part2(n):
                def ev_g2(m, b):
                    act(S2[:, m, :], banks[b][:, :], AF.Sigmoid, [bank_tok[b]], [t_s2])
                fm_proj(w_in, SOFF_G + D + n * 512, 0, KC, hrhs, [tk["hT"]], ev_g2)

                def ev_yp(m, b):
                    tt("dve", S2[:, m, :], S2[:, m, :], banks[b][:, :], ALU.mult, [bank_tok[b], t_s2], [t_s2])
                    tt("dve", merged[:, n * 4 + m, :], S2[:, m, :], S1[:, m, :], ALU.add, [t_s1, t_s2], [t_mg[n]])
                fm_proj(w_branch, n * 512, DV, 8, lambda k: o_pool[:, k, :], [t_op], ev_yp)

            merge_part2(0)
            gb2 = load_gb(1, "R1", A_R1) if stages >= 2 else None
            for n in range(1, 4):
                merge_part1(n)
                merge_part2(n)
            lut_prefetch()
            for n in range(4):
                tm_proj(w_out, n * 512, 0, KC, lambda k, su: merged[:, k, su * 128:(su + 1) * 128], [], resid_evac(n),
                        lhs_tok_fn=lambda k: [t_mg[k // 4]])
            if "x1" in dbg and ti == 0:
                dump("x1", xt[:], x_tok, [128, NSUB, D])

            if stages >= 2:
                cur_stage[0] = 2
                gb, t_gb = gb2
                gb3 = load_gb(3, "R6", A_R6) if stages >= 3 else None
                norm_to_fm(lambda su: xt[:, su, :], x_tok, NSUB, gb, t_gb, hT, tk["hT"], TT, warm=0)
                t_qc = retok("R2", ["qcT"])
                qcT = av(A_R2, [KC, TT], BF16)
                for n in range(4):
                    def ev_q(m, b, n=n):
                        cp_any(qcT[:, n * 4 + m, :], banks[b][:, :], [bank_tok[b]], [t_qc])
                    fm_proj(w_cq, n * 512, 0, KC, hrhs, [tk["hT"]], ev_q)
                t_o2 = retok("R3", ["oT2_0", "oT2_1", "oT2_2", "oT2_3"])
                oT2 = av(A_R3, [KC, TT], BF16)
                t_pt = retok("R4", ["PT0", "PT1"])
                PTs = [av(A_R4 + i * 2048, [2, TT], BF16) for i in range(2)]
                t_rs = retok("R5", ["rs0", "rs1"])
                rss = [av(A_R5 + i * 2048, [TT], F32) for i in range(2)]

                def scores(h):
                    PT, tp = PTs[h % 2], t_pt[h % 2]
                    for mc in range(2):
                        b = next_bank()
                        for hc in range(4):
                            mm(banks[b][:, :], KmT[:, h * 4 + hc, mc * 128:(mc + 1) * 128], qcT[:, h * 4 + hc, :], hc == 0, hc == 3,
                               [tk["KmT"], t_qc], [bank_tok[b]])
                        act(PT[:, mc, :], banks[b][:, :], AF.Exp, [bank_tok[b]], [tp], scale=512.0 ** -0.5)

                def pv(h):
                    PT, tp = PTs[h % 2], t_pt[h % 2]
                    rs, tr_ = rss[h % 2], t_rs[h % 2]
                    b = next_bank()
                    for mc in range(2):
                        mm(banks[b][:, :], ones_bf[:], PT[:, mc, :], mc == 0, mc == 1, [tk["ones_bf"], tp], [bank_tok[b]])
                    act(rs, banks[b][:, :], AF.Ln, [bank_tok[b]], [tr_])
                    act(rs, rs, AF.Exp, [tr_], [tr_], scale=-1.0)
                    for hc in range(4):
                        b = next_bank()
                        for mc in range(2):
                            mm(banks[b][:, :], Vm[:, mc, h * 512 + hc * 128: h * 512 + (hc + 1) * 128], PT[:, mc, :], mc == 0, mc == 1,
                               [tk["Vm"], tp], [bank_tok[b]])
                        tt("dve", oT2[:, h * 4 + hc, :], banks[b][:, :], rs, ALU.mult, [bank_tok[b], tr_], [t_o2[h]])

                scores(0)
                for h in range(NH):
                    if h + 1 < NH:
                        scores(h + 1)
                    pv(h)
                for n in range(4):
                    tm_proj(w_co, n * 512, 0, KC, lambda k, su: oT2[:, k, su * 128:(su + 1) * 128], [], resid_evac(n),
                            lhs_tok_fn=lambda k: [t_o2[k // 4]])
                if "x2" in dbg and ti == 0:
                    dump("x2", xt[:], x_tok, [128, NSUB, D])

            if stages >= 3:
                cur_stage[0] = 3
                gb, t_gb = gb3
                norm_to_fm(lambda su: xt[:, su, :], x_tok, NSUB, gb, t_gb, hT, tk["hT"], TT, warm=0)
                olds_ = region_toks.get("R1", []) + region_toks.get("R2", []) + region_toks.get("R3", [])
                t_as = [tok_after(f"a3_{g}", olds_) for g in range(11)]
                region_toks["R1"] = list(t_as)
                region_toks["R2"] = list(t_as)
                region_toks["R3"] = list(t_as)
                a3 = av(A_R1, [NFC, TT], BF16)
                PCW = lambda c0: params[:, c0:c0 + 2 * NFC]
                h0 = chalo[:, :, 0]
                h1 = chalo[:, :, 1]
                rd = [tk["chalo"], tk["params"]]
                tt("dve", cb0[:], PCW(PC_CW1), h1, ALU.mult, rd, [tk["cb0"]])
                tt("dve", cb0[:], cb0[:], PCW(PC_CB), ALU.add, [tk["cb0"], tk["params"]], [tk["cb0"]])
                tt("dve", cb1[:], PCW(PC_CW0), h0, ALU.mult, rd, [tk["cb1"]])
                tt("dve", cb0[:], cb0[:], cb1[:], ALU.add, [tk["cb0"], tk["cb1"]], [tk["cb0"]])
                tt("dve", cb1[:], PCW(PC_CW0), h1, ALU.mult, rd + [tk["cb0"]], [tk["cb1"]])
                tt("dve", cb1[:], cb1[:], PCW(PC_CB), ALU.add, [tk["cb1"], tk["params"]], [tk["cb1"]])
                t_acc = retok("R6", ["accg0", "accg1", "accv0", "accv1"])
                accs = [av(A_R6 + i * 2048, [TT], F32) for i in range(4)]
                t_sg4 = retok("R4", ["sg0", "sg1", "sg2", "sg3"])
                sg4 = av(A_R4, [4, TT], F32)

                def conv(ai, b, fc):
                    acc, t_ac = accs[ai], t_acc[ai]
                    bk, tb = banks[b], bank_tok[b]
                    pr = [tb, tk["params"]]
                    act(acc[:, 2:TT], bk[:, 2:TT], AF.Identity, pr, [t_ac], scale=pcol(PC_CW2 + fc), bias=pcol(PC_CB + fc))
                    P.op("act", lambda e: e.activation(out=acc[:, 0:1], in_=bk[:, 0:1], func=AF.Identity, scale=pcol(PC_CW2 + fc),
                                                       bias=cb0[:, fc:fc + 1]), pr + [tk["cb0"]], [t_ac], nowaw=True)
                    P.op("act", lambda e: e.activation(out=acc[:, 1:2], in_=bk[:, 1:2], func=AF.Identity, scale=pcol(PC_CW2 + fc),
                                                       bias=cb1[:, fc:fc + 1]), pr + [tk["cb1"]], [t_ac], nowaw=True)
                    cp("act", chalo[:, fc, :], bk[:, TT - 2:TT], [tb], [tk["chalo"]], nowaw=True)
                    stt("dve", acc[:, 1:TT], bk[:, 0:TT - 1], pcol(PC_CW1 + fc), acc[:, 1:TT], ALU.mult, ALU.add, pr + [t_ac], [t_ac])
                    stt("dve", acc[:, 2:TT], bk[:, 0:TT - 2], pcol(PC_CW0 + fc), acc[:, 2:TT], ALU.mult, ALU.add, pr + [t_ac], [t_ac])

                for s_ in range(11):
                    def ev_gate(m, b, s_=s_):
                        fc = s_ * 4 + m
                        conv(m % 2, b, fc)
                        if m > 0:
                            act(sg4[:, m - 1, :], accs[(m - 1) % 2], AF.Silu, [t_acc[(m - 1) % 2]], [t_sg4[m - 1]])
                        if m == 3:
                            act(sg4[:, 3, :], accs[1], AF.Silu, [t_acc[1]], [t_sg4[3]])
                    fm_proj(w_up, s_ * 512, 0, KC, hrhs, [tk["hT"]], ev_gate)

                    def ev_val(m, b, s_=s_):
                        fc = s_ * 4 + m
                        conv(2 + m % 2, b, NFC + fc)
                        P.op("dve", lambda e: e.tensor_tensor(out=a3[:, fc, :], in0=sg4[:, m, :], in1=accs[2 + m % 2], op=ALU.mult),
                             [t_sg4[m], t_acc[2 + m % 2]], [t_as[s_]], nowaw=True)
                    fm_proj(w_up, DFF + s_ * 512, 0, KC, hrhs, [tk["hT"]], ev_val)
                for n in range(4):
                    tm_proj(w_down, n * 512, 0, NFC, lambda k, su: a3[:, k, su * 128:(su + 1) * 128], [], resid_evac(n),
                            lhs_tok_fn=lambda k: [t_as[k // 4]])
                    if n == 0:
                        lut_prefetch()
                        gbF = load_gb(4, "R4", A_R4)
                        if ti + 1 < ntiles:
                            gb_next[0] = load_gb(0, "R6", A_R6)

            gb, t_gb = gbF if stages >= 3 else load_gb(4, "R4", A_R4)
            olds_ = region_toks.get("R1", []) + region_toks.get("R2", [])
            t_sts = [tok_after(f"stage{su}", olds_) for su in range(NSUB)]
            region_toks["R1"] = list(t_sts)
            region_toks["R2"] = list(t_sts)
            stg = av(A_R1, [NSUB, D], F32)
            for su in range(NSUB):
                P.op("act", lambda e, su=su: e.activation(out=xn[:], in_=xt[:, su, :], func=AF.Square, accum_out=ssq[:, 4 + su:5 + su]),
                     [x_tok[su]], [tk["xn"], tk_ssq[4 + su]], nowaw=True)
                act(rstd[:, 4 + su:5 + su], ssq[:, 4 + su:5 + su], AF.Ln, [tk_ssq[4 + su]], [tk_rstd[4 + su]], bias=EPS, scale=1.0 / D)
                act(rstd[:, 4 + su:5 + su], rstd[:, 4 + su:5 + su], AF.Exp, [tk_rstd[4 + su]], [tk_rstd[4 + su]], scale=-0.5)
                P.op("dve", lambda e, su=su: e.scalar_tensor_tensor(out=stg[:, su, :], in0=xt[:, su, :], scalar=rstd[:, 4 + su:5 + su], in1=gb,
                                                                     op0=ALU.mult, op1=ALU.mult),
                     [x_tok[su], tk_rstd[4 + su], t_gb], [t_sts[su]])
                if ti + 1 < ntiles:
                    qn = "sp" if ti + 1 <= 2 else "pool"
                    dma(qn, xt[:, su, :], x_d[t0 + TT + su * 128: t0 + TT + (su + 1) * 128, :], [], [x_tok[su]], s_x[su])
                out_events.append(dma(mq[0], y_d[t0 + su * 128: t0 + (su + 1) * 128, :], stg[:, su, :], [t_sts[su]], [], s_out[su]))

        P.wait_events("sp", out_events)
        P.emit()
    return nc, dbg_out


def host_prep(inputs):
    f = lambda a: np.ascontiguousarray(np.asarray(a, dtype=np.float32))
    sq = lambda name: f(inputs[name])[0]
    col = lambda vec: vec.reshape(-1, 128).T
    params = np.zeros((128, PC_N), np.float32)
    params[:, PC_GMIX:PC_GMIX + 16] = col(sq("g_mix"))
    params[:, PC_GCROSS:PC_GCROSS + 16] = col(sq("g_cross"))
    params[:, PC_GMEM:PC_GMEM + 16] = col(sq("g_mem"))
    params[:, PC_GFFN:PC_GFFN + 16] = col(sq("g_ffn"))
    params[:, PC_GGLA:PC_GGLA + 8] = col(sq("g_gla"))
    params[:, PC_PSCALE:PC_PSCALE + 8] = col(sq("pool_scale"))
    cw = sq("conv_w")
    params[:, PC_CW0:PC_CW0 + 88] = col(cw[0])
    params[:, PC_CW1:PC_CW1 + 88] = col(cw[1])
    params[:, PC_CW2:PC_CW2 + 88] = col(cw[2])
    params[:, PC_CB:PC_CB + 88] = col(sq("conv_b"))
    consts = np.zeros((128, CC_N), np.float32)
    consts[:, CC_IDENT:CC_IDENT + 128] = np.eye(128, dtype=np.float32)
    j = np.arange(128)[:, None]
    i = np.arange(128)[None, :]
    m01 = (j <= i).astype(np.float32)
    consts[:, CC_MASK4:CC_MASK4 + 512] = np.tile(m01, (1, 4))
    consts[:, CC_TRIU:CC_TRIU + 128] = m01 * np.float32(-1.0 / 16.0)
    consts[:, CC_ONES:CC_ONES + 128] = 1.0
    for g_ in range(4):
        w_ = 2 << g_
        consts[:, CC_INVC + g_ * 16: CC_INVC + (g_ + 1) * 16] = 1.0 / np.minimum(np.arange(16) + 1, w_).astype(np.float32)
    def to_slabs(W):
        K, N = W.shape
        nkh = (K + 1023) // 1024
        if K != nkh * 1024:
            W = np.concatenate([W, np.zeros((nkh * 1024 - K, N), np.float32)], 0)
        W = W.reshape(nkh, 8, 128, N // 512, 512).transpose(3, 0, 2, 1, 4)
        return np.ascontiguousarray(W).reshape((N // 512) * nkh, 128, 8 * 512)

    win = sq("w_in")
    shared = {
        "w_in": to_slabs(np.concatenate([win[:, :OFF_A], win[:, OFF_P:]], 1)),
        "wa_in": np.ascontiguousarray(win[:, OFF_A:OFF_P]),
        "wa2": np.concatenate([sq("w_a2"), sq("b_a")[None, :]], 0),
        "w_pool": sq("w_pool"), "w_branch": to_slabs(sq("w_branch")), "w_out": to_slabs(sq("w_out")),
        "w_cq": to_slabs(sq("w_cq")), "w_ckv": to_slabs(sq("w_ckv")), "w_co": to_slabs(sq("w_co")),
        "w_up": to_slabs(sq("w_up")), "w_down": to_slabs(sq("w_down")),
        "gvec": np.stack([sq("g_mix"), sq("g_cross"), sq("g_mem"), sq("g_ffn"), f(inputs["g_final"])], 0),
        "params": params, "consts": consts,
    }
    return shared


def kernel(**inputs):
    shared = host_prep(inputs)
    x = np.asarray(inputs["x"], dtype=np.float32)
    mem = np.asarray(inputs["mem"], dtype=np.float32)
    nb = x.shape[0]
    nc, _ = build()
    in_maps = []
    for b in range(nb):
        m = dict(shared)
        m["x"] = np.ascontiguousarray(x[b])
        m["mem"] = np.ascontiguousarray(mem[b])
        in_maps.append(m)
    res = run_bass_kernel_spmd(nc, in_maps, core_ids=list(range(nb)))
    return np.stack([np.asarray(r["y"], dtype=np.float32) for r in res.results], 0)
```
